# Optimizing a Trainium2 kernel written in Bass

```python
import math
import jax, jax.numpy as jnp
from jax import lax
import numpy as np

D_MODEL = 1024
BATCH = 4
SEQ = 4096
DEPTH = 2

CHUNK = 64
HEAD_DIM = 64
D_CONV = D_MODEL // 2
N_SB_HEADS = 8
D_SB = N_SB_HEADS * HEAD_DIM
D_MIX = D_CONV + D_SB
CONV_WIDTH = 3
PLE_DIM = 256
Q_BLOCK = 128
N_IN = 4 * D_CONV + 4 * D_SB
EPS = 1e-6

kernel_name = "hybrid_shortconv_stickbreaking_ple"


def rmsnorm(x, g):
    xf = x.astype(jnp.float32)
    y = xf * lax.rsqrt(jnp.mean(xf * xf, axis=-1, keepdims=True) + EPS)
    return (y * g.astype(jnp.float32)).astype(x.dtype)


def group_rmsnorm(y, g, group):
    shp = y.shape
    yf = y.astype(jnp.float32).reshape(shp[:-1] + (shp[-1] // group, group))
    yf = yf * lax.rsqrt(jnp.mean(yf * yf, axis=-1, keepdims=True) + EPS)
    return (yf.reshape(shp) * g.astype(jnp.float32)).astype(y.dtype)


def causal_dwconv(u, w, b):
    s = u.shape[1]
    up = jnp.pad(u, ((0, 0), (CONV_WIDTH - 1, 0), (0, 0)))
    y = b
    for j in range(CONV_WIDTH):
        y = y + up[:, j:j + s, :] * w[j]
    return y


def stick_breaking_block(q_blk, k_pre, v_pre, t0):
    dh = q_blk.shape[-1]
    z = jnp.einsum('bqhd,bkhd->bhqk', q_blk.astype(jnp.float32), k_pre.astype(jnp.float32)) / math.sqrt(dh)
    qb, kl = q_blk.shape[1], k_pre.shape[1]
    t_idx = t0 + jnp.arange(qb)[:, None]
    s_idx = jnp.arange(kl)[None, :]
    mask = s_idx < t_idx
    log_1m = jnp.where(mask, jax.nn.log_sigmoid(-z), 0.0)
    rem = lax.cumsum(log_1m, axis=3, reverse=True) - log_1m
    a = jnp.where(mask, jnp.exp(jax.nn.log_sigmoid(z) + rem), 0.0)
    out = jnp.einsum('bhqk,bkhd->bqhd', a, v_pre.astype(jnp.float32))
    return out.astype(q_blk.dtype)


def stick_breaking_attention(q, k, v):
    s = q.shape[1]
    outs = []
    for blk in range(s // Q_BLOCK):
        t0 = blk * Q_BLOCK
        kend = t0 + Q_BLOCK
        outs.append(stick_breaking_block(q[:, t0:kend], k[:, :kend], v[:, :kend], t0))
    return jnp.concatenate(outs, axis=1)


def setup_inputs(seed: int = 0) -> dict:
    key = jax.random.key(seed)
    ks = jax.random.split(key, 14)
    f32 = jnp.float32
    x = jax.random.normal(ks[0], (BATCH, SEQ, D_MODEL), f32)
    p = jax.random.normal(ks[1], (DEPTH, BATCH, SEQ, PLE_DIM), f32)
    norm_g = 1.0 + 0.02 * jax.random.normal(ks[2], (DEPTH, D_MODEL), f32)
    w_in = jax.random.normal(ks[3], (DEPTH, D_MODEL, N_IN), f32) * D_MODEL ** -0.5
    conv_w = jax.random.normal(ks[4], (DEPTH, CONV_WIDTH, D_CONV), f32) * CONV_WIDTH ** -0.5
    conv_b = 0.02 * jax.random.normal(ks[5], (DEPTH, D_CONV), f32)
    branch_g = 1.0 + 0.02 * jax.random.normal(ks[6], (DEPTH, D_MIX), f32)
    w_out = jax.random.normal(ks[7], (DEPTH, D_MIX, D_MODEL), f32) * D_MIX ** -0.5
    ple_norm_g = 1.0 + 0.02 * jax.random.normal(ks[8], (DEPTH, D_MODEL), f32)
    w_pg = jax.random.normal(ks[9], (DEPTH, D_MODEL, D_MODEL), f32) * D_MODEL ** -0.5
    b_pg = 0.02 * jax.random.normal(ks[10], (DEPTH, D_MODEL), f32)
    w_pe = jax.random.normal(ks[11], (DEPTH, PLE_DIM, D_MODEL), f32) * PLE_DIM ** -0.5
    final_g = 1.0 + 0.02 * jax.random.normal(ks[12], (D_MODEL,), f32)
    return {"x": x, "p": p, "norm_g": norm_g, "w_in": w_in, "conv_w": conv_w,
            "conv_b": conv_b, "branch_g": branch_g, "w_out": w_out,
            "ple_norm_g": ple_norm_g, "w_pg": w_pg, "b_pg": b_pg, "w_pe": w_pe,
            "final_g": final_g}


def reference(x, p, norm_g, w_in, conv_w, conv_b, branch_g, w_out,
              ple_norm_g, w_pg, b_pg, w_pe, final_g):
    bsz, s, _ = x.shape
    for i in range(DEPTH):
        h = rmsnorm(x, norm_g[i])
        proj = h @ w_in[i]
        c_b, c_c, c_h, c_z, q, k, v, a_z = jnp.split(
            proj, [D_CONV, 2 * D_CONV, 3 * D_CONV, 4 * D_CONV,
                   4 * D_CONV + D_SB, 4 * D_CONV + 2 * D_SB, 4 * D_CONV + 3 * D_SB], axis=-1)
        y_c = c_b * causal_dwconv(c_c * c_h, conv_w[i], conv_b[i])
        qh = q.reshape(bsz, s, N_SB_HEADS, HEAD_DIM)
        kh = k.reshape(bsz, s, N_SB_HEADS, HEAD_DIM)
        vh = v.reshape(bsz, s, N_SB_HEADS, HEAD_DIM)
        y_a = stick_breaking_attention(qh, kh, vh).reshape(bsz, s, D_SB)
        y = group_rmsnorm(jnp.concatenate([y_c, y_a], axis=-1), branch_g[i], HEAD_DIM)
        y = y * jax.nn.silu(jnp.concatenate([c_z, a_z], axis=-1))
        x = x + y @ w_out[i]
        gate = jax.nn.sigmoid(rmsnorm(x, ple_norm_g[i]) @ w_pg[i] + b_pg[i])
        x = x + gate * (p[i] @ w_pe[i])
    return rmsnorm(x, final_g)
```

```python
import contextlib
import numpy as np
import concourse.bass as bass
import concourse.mybir as mybir
from concourse.bass_utils import run_bass_kernel_spmd

F32 = mybir.dt.float32
BF16 = mybir.dt.bfloat16
AF = mybir.ActivationFunctionType
ALU = mybir.AluOpType

DEPTH = 2
S = 4096
HALF = 2048
D = 1024
NEG = -30000.0
EPS = 1e-6
NCORES = 8


class Sem:
    def __init__(self, name):
        self.name = name
        self.h = None
        self.val = 0


class Q:
    def __init__(self, name):
        self.name = name
        self.ops = []
        self.sem = Sem("q" + name)


class Buf:
    def __init__(self, name):
        self.name = name
        self.w = {}
        self.r = {}
        self.dsem = Sem("d" + name)


def merge(a, b):
    for k, v in b.items():
        if a.get(k, 0) < v:
            a[k] = v


class Sched:
    def __init__(self):
        self.pe = Q("pe")
        self.act = Q("act")
        self.dve = Q("dve")
        self.pool = Q("pool")
        self.sp = Q("sp")
        self.queues = [self.pe, self.act, self.dve, self.pool, self.sp]
        self.bufs = []
        self.extra_sems = []

    def buf(self, name):
        b = Buf(name)
        self.bufs.append(b)
        return b

    def op(self, q, fn, reads=(), writes=(), kind="c", dsem=None, merge_w=False, extra=None):
        waits = {}
        for b in reads:
            merge(waits, b.w)
        for b in writes:
            merge(waits, b.r)
            merge(waits, b.w)
        if extra:
            merge(waits, extra)
        if q is self.pe:
            waits.pop(self.pe.sem, None)
        if kind == "c":
            q.sem.val += 1
            tok = {q.sem: q.sem.val}
            sig = (q.sem, 1)
        elif kind == "dma":
            dsem.val += 16
            tok = {dsem: dsem.val}
            sig = (dsem, 16)
        else:
            dsem.val += 1
            tok = {dsem: dsem.val}
            sig = (dsem, 1)
        for b in writes:
            if merge_w:
                merge(b.w, tok)
            else:
                b.w = dict(tok)
            b.r = {}
        for b in reads:
            merge(b.r, tok)
        q.ops.append((fn, waits, sig))
        return tok

    def all_tokens(self):
        t = {}
        for q in self.queues:
            if q.sem.val:
                t[q.sem] = q.sem.val
        for b in self.bufs:
            if b.dsem.val:
                t[b.dsem] = b.dsem.val
        for s in self.extra_sems:
            if s.val:
                t[s] = s.val
        return t

    def barrier(self, extra=None, exclude=()):
        t = self.all_tokens()
        for sem in exclude:
            t.pop(sem, None)
        if extra:
            merge(t, extra)
        for q in self.queues:
            w = dict(t)
            q.ops.append((None, w, None))

    def sems(self):
        out = [q.sem for q in self.queues]
        out += [b.dsem for b in self.bufs if b.dsem.val]
        out += [s for s in self.extra_sems]
        return out


def I(name, *args, **kw):
    return (name, args, kw)


def replay(q, eng):
    seen = {}
    for fn, waits, sig in q.ops:
        for sem, val in waits.items():
            if seen.get(sem, 0) < val:
                eng.wait_ge(sem.h, val)
                seen[sem] = val
        if fn is None:
            continue
        if callable(fn):
            ins = fn(eng)
        else:
            ins = getattr(eng, fn[0])(*fn[1], **fn[2])
        if sig is not None:
            ins.then_inc(sig[0].h, sig[1])


def build_program(debug=False, stop_after=None):
    nc = bass.Bass("TRN2", target_bir_lowering=False)
    sc = Sched()
    PE, ACT, DVE, POOL, SP = sc.pe, sc.act, sc.dve, sc.pool, sc.sp

    def ext_in(name, shape, dt=F32):
        return nc.dram_tensor(name, shape, dt, kind="ExternalInput")

    xT_d = ext_in("xT", [128, 8, HALF])
    pT_d = ext_in("pT", [DEPTH, 128, 2, HALF])
    win_d = ext_in("win", [DEPTH, 128, 8, 2048])
    wout_d = ext_in("wout", [DEPTH, 128, 8, 1024])
    wpg_d = ext_in("wpg", [DEPTH, 128, 8, 1024])
    wpe_d = ext_in("wpe", [DEPTH, 128, 2, 1024])
    NV = 8 + 8 + 8 + 6 + 2 + 2 + 2
    vec_d = ext_in("vec", [128, DEPTH * NV + 8])
    cst_d = ext_in("cst", [128, 6 * 128])
    outT_d = nc.dram_tensor("outT", [128, 8, HALF], F32, kind="ExternalOutput")

    hb_d = [nc.dram_tensor(f"hb{i}", [D, 512], BF16) for i in range(4)]
    hg_d = [nc.dram_tensor(f"hg{i}", [2 * D, 512], BF16) for i in range(4)]
    YSH = [(256, S), (128, S)] + [(128, 1024)] * 4
    NY = len(YSH)
    yb_d = [nc.dram_tensor(f"yb{i}", [YSH[i][0], YSH[i][1]], BF16) for i in range(NY)]
    yg_d = [nc.dram_tensor(f"yg{i}", [2 * YSH[i][0], YSH[i][1]], BF16) for i in range(NY)]
    gz_d = nc.dram_tensor("gz", [256, S], BF16)
    dbg = {}
    if debug:
        dbg["hg"] = [nc.dram_tensor(f"dbg_hg{i}", [2 * D, 512], BF16, kind="ExternalOutput") for i in range(4)]
        dbg["yg"] = [nc.dram_tensor(f"dbg_yg{i}", [2 * YSH[i][0], YSH[i][1]], BF16, kind="ExternalOutput")
                     for i in range(NY)]

    hb_b = [sc.buf(f"hb{i}") for i in range(4)]
    hg_b = [sc.buf(f"hg{i}") for i in range(4)]
    yb_b = [sc.buf(f"yb{i}") for i in range(NY)]
    yg_b = [sc.buf(f"yg{i}") for i in range(NY)]
    gz_b = sc.buf("gz")
    out_b = sc.buf("outd")
    cc_sem = Sem("cc")
    sc.extra_sems.append(cc_sem)

    with contextlib.ExitStack() as es:
        def sb(name, shape, dt):
            return es.enter_context(nc.sbuf_tensor(name, shape, dt))

        def ps(name):
            return es.enter_context(nc.psum_tensor(name, [128, 512], F32))

        xT = sb("xT_s", [128, 8, HALF], F32)
        wbuf = sb("wbuf", [128, 18432], BF16)
        NWS = 4
        wst = [sb(f"wst{i}", [128, 512], F32) for i in range(NWS)]
        big = sb("bigbf", [128, 24576], BF16)
        bfa = sb("bfa", [128, 24, 512], BF16)
        NF = 12
        FA = sb("FA", [128, NF * 514], F32)
        Fp = [FA[:, i * 514:(i + 1) * 514] for i in range(NF)]

        def Fpair(i):
            return FA[:, i * 514:(i + 2) * 514].rearrange("p (a c) -> p a c", a=2)
        NCST = 6 * 128
        cbf = sb("cbf", [128, NCST], BF16)
        vec = sb("vecs", [128, DEPTH * NV + 8], F32)
        nbpg = sb("nbpg", [128, DEPTH * 8 + 1], F32)
        epscol = nbpg[:, DEPTH * 8:DEPTH * 8 + 1]
        P2 = [es.enter_context(nc.psum_tensor(f"pp{i}", [128, 1024], F32)) for i in range(4)]
        P = [P2[i // 2][:, (i % 2) * 512:(i % 2 + 1) * 512] for i in range(8)]

        def Ppair(i):
            return P2[i][:, :].rearrange("p (a c) -> p a c", a=2)
        hTt = [bfa[:, 0:8, :], bfa[:, 8:16, :]]
        Bp = [bfa[:, 16 + i, :] for i in range(8)]
        Bx = [bfa[:, i, :] for i in range(8)]

        b_xT = [sc.buf(f"xT{t}") for t in range(4)]
        b_wlo = sc.buf("wbuf_lo")
        b_whi = sc.buf("wbuf_hi")
        b_wst = [sc.buf(f"wst{i}") for i in range(NWS)]
        b_hTt = [sc.buf(f"hTt{i}") for i in range(2)]
        b_F = [sc.buf(f"F{i}") for i in range(NF)]
        b_B = [sc.buf(f"B{i}") for i in range(8)]
        b_Bx = [sc.buf(f"Bx{i}") for i in range(8)]
        b_P = [sc.buf(f"P{i}") for i in range(8)]
        b_cst = sc.buf("cst")
        b_vec = sc.buf("vec")
        b_q = sc.buf("qT")
        b_k = sc.buf("kT")
        b_v = sc.buf("V")
        b_yT = [sc.buf(f"yT{i}") for i in range(2)]
        b_hp = sc.buf("hp")
        b_pTb = sc.buf("pTb")
        b_sq2 = sc.buf("sq2")
        b_hp2 = sc.buf("hp2")
        b_swst = sc.buf("swst")

        qT = big[:, 0:8192].rearrange("p (a t) -> p a t", a=2)
        kT = big[:, 8192:16384].rearrange("p (a t) -> p a t", a=2)
        V = big[:, 16384:24576].rearrange("p (n c) -> p n c", c=256)
        yTt = [big[:, i * 4096:(i + 1) * 4096].rearrange("p (c t) -> p c t", c=8) for i in range(2)]
        hp = big[:, 8192:12288].rearrange("p (c t) -> p c t", c=8)
        pTb = big[:, 16384:17408].rearrange("p (c t) -> p c t", c=2)

        triN = cbf[:, 0:128]
        triC = cbf[:, 128:256]
        ident = cbf[:, 256:384]
        gsum = cbf[:, 384:512]
        onesms = cbf[:, 512:640]
        mask01 = cbf[:, 640:768]

        def vcol(layer, off, n=1):
            base = layer * NV + off
            return vec[:, base:base + n]
        V_NORMG, V_PLEN, V_BPG, V_CW, V_CB, V_BGC, V_BGA = 0, 8, 16, 24, 30, 32, 34
        fgcol = lambda k: vec[:, DEPTH * NV + k: DEPTH * NV + k + 1]

        sc.op(SP, I("dma_start", out=vec[:], in_=vec_d[:, :]), writes=[b_vec], kind="dma", dsem=b_vec.dsem)
        csrc = [(0, 512), (512, 768)]
        for i, (a, b) in enumerate(csrc):
            n = b - a
            sc.op(SP, I("dma_start", out=Fp[i][:, 0:n], in_=cst_d[:, a:b]),
                  writes=[b_F[i]], kind="dma", dsem=b_F[i].dsem)
            sc.op(DVE, I("tensor_copy", out=cbf[:, a:b], in_=Fp[i][:, 0:n]),
                  reads=[b_F[i]], writes=[b_cst], merge_w=True)
        xtok = None
        for t4 in range(4):
            xtok = sc.op(SP, I("dma_start", out=xT[:, :, t4 * 512:(t4 + 1) * 512],
                               in_=xT_d[:, :, t4 * 512:(t4 + 1) * 512]),
                         writes=[b_xT[t4]], kind="dma", dsem=b_xT[t4].dsem, extra=xtok)
        for l_ in range(DEPTH):
            sc.op(DVE, I("tensor_scalar", out=nbpg[:, l_ * 8:(l_ + 1) * 8], in0=vcol(l_, V_BPG, 8), scalar1=-1.0,
                         scalar2=None, op0=ALU.mult), reads=[b_vec], writes=[b_vec], merge_w=True)
        sc.op(DVE, I("memset", epscol, EPS), reads=[b_vec], writes=[b_vec], merge_w=True)

        wst_i = [0]

        def w_piece(src_ap, dst_off, scale_col, eng, wide=False, fslots=None, defer=None):
            fsl = fslots if fslots is not None else (list(range(2, 12)) if wide else [])
            nsl = NWS + len(fsl)
            sl = wst_i[0] % nsl
            wst_i[0] += 1
            if sl < NWS:
                stg, stb = wst[sl][:, :], b_wst[sl]
            else:
                stg, stb = Fp[fsl[sl - NWS]][:, 0:512], b_F[fsl[sl - NWS]]
            sc.op(SP, I("dma_start", out=stg, in_=src_ap), writes=[stb], kind="dma", dsem=stb.dsem)
            dst = wbuf[:, dst_off:dst_off + 512]
            rd = [stb] + ([b_vec] if scale_col is not None else [])
            if eng is ACT:
                if scale_col is None:
                    ins = I("copy", out=dst, in_=stg)
                else:
                    ins = I("mul", out=dst, in_=stg, mul=scale_col)
            else:
                if scale_col is None:
                    ins = I("tensor_copy", out=dst, in_=stg)
                else:
                    ins = I("tensor_scalar", out=dst, in0=stg, scalar1=scale_col, scalar2=None,
                            op0=ALU.mult)
            def cast():
                sc.op(eng, ins, reads=rd, writes=[b_wlo if dst_off < 8192 else b_whi], merge_w=True)
            if defer is None:
                cast()
            else:
                defer.append(cast)

        def win_pieces(layer):
            out = []
            for k in range(8):
                for c0 in range(0, 2048, 512):
                    out.append((win_d[layer, :, k, c0:c0 + 512], k * 2048 + c0, vcol(layer, V_NORMG + k)))
            return out

        def wd_pieces(layer):
            out = []
            for k in range(8):
                for c0 in range(0, 1024, 512):
                    out.append((wout_d[layer, :, k, c0:c0 + 512], k * 1024 + c0, None))
            for k in range(8):
                for c0 in range(0, 1024, 512):
                    out.append((wpg_d[layer, :, k, c0:c0 + 512], 8192 + k * 1024 + c0, vcol(layer, V_PLEN + k)))
            for j in range(2):
                for c0 in range(0, 1024, 512):
                    out.append((wpe_d[layer, :, j, c0:c0 + 512], 16384 + j * 1024 + c0, None))
            return out

        def load_win(layer, lo=0, hi=32, engs=None):
            engs = engs or [ACT, DVE]
            for i, (src, off, scol) in enumerate(win_pieces(layer)):
                if lo <= i < hi:
                    w_piece(src, off, scol, engs[i % len(engs)], wide=True)

        win = wbuf[:, 0:16384].rearrange("p (k c) -> p k c", k=8)
        wout = wbuf[:, 0:8192].rearrange("p (k c) -> p k c", k=8)
        wpg = wbuf[:, 8192:16384].rearrange("p (k c) -> p k c", k=8)
        wpe = wbuf[:, 16384:18432].rearrange("p (k c) -> p k c", k=2)

        half_cache = {}

        def rstd_from(psb, pst, fb, f):
            sc.op(ACT, I("activation", out=f[:, 0:512], in_=pst[:, :], func=AF.Ln, bias=epscol),
                  reads=[psb, b_vec], writes=[fb])
            sc.op(ACT, I("activation", out=f[:, 0:512], in_=f[:, 0:512], func=AF.Exp, scale=-0.5),
                  reads=[fb], writes=[fb])

        def sigmoid_act(pst, psb, f, fb, nbias=None):
            kw = {} if nbias is None else {"bias": nbias}
            rd = [psb] + ([] if nbias is None else [b_vec])
            sc.op(ACT, I("activation", out=f[:, 0:512], in_=pst[:, :], func=AF.Exp, scale=-1.0, **kw),
                  reads=rd, writes=[fb])
            sc.op(ACT, I("activation", out=f[:, 0:512], in_=f[:, 0:512], func=AF.Ln, bias=1.0),
                  reads=[fb], writes=[fb])
            sc.op(ACT, I("activation", out=f[:, 0:512], in_=f[:, 0:512], func=AF.Exp, scale=-1.0),
                  reads=[fb], writes=[fb])

        def norm_sq(t4, sq, sq_b):
            t0 = t4 * 512
            merge(b_sq2.r, sq_b.r)
            merge(b_sq2.w, sq_b.w)
            for hf in range(2):
                hb_ = sq_b if hf == 0 else b_sq2
                sc.op(ACT, I("activation", out=sq[:, hf * 4:(hf + 1) * 4, :], in_=xT[:, hf * 4:(hf + 1) * 4, t0:t0 + 512],
                             func=AF.Square), reads=[b_xT[t4]], writes=[hb_])

        def norm_ms(sq, sq_b, psi):
            for k in range(8):
                hb_ = sq_b if k < 4 else b_sq2
                tok = sc.op(PE, I("matmul", P[psi][:, :], lhsT=onesms, rhs=sq[:, k, :], start=(k == 0),
                                  stop=(k == 7)), reads=[hb_, b_cst], writes=[b_P[psi]])
                if k >= 4:
                    merge(sq_b.r, tok)
            merge(sq_b.w, b_sq2.w)

        def norm_front(t4, sq, sq_b, psi, fi):
            norm_sq(t4, sq, sq_b)
            norm_ms(sq, sq_b, psi)
            rstd_from(b_P[psi], P[psi], b_F[fi], Fp[fi])

        def norm_back(t4, dst, dst_b, fi):
            t0 = t4 * 512
            for k in range(8):
                sc.op(DVE, I("tensor_tensor", out=dst[:, k, :], in0=xT[:, k, t0:t0 + 512],
                             in1=Fp[fi][:, 0:512], op=ALU.mult),
                      reads=[b_xT[t4], b_F[fi]], writes=[dst_b], merge_w=(k > 0))

        def norm_tile(t4, dst, dst_b, sq, sq_b, psi, fi):
            norm_front(t4, sq, sq_b, psi, fi)
            norm_back(t4, dst, dst_b, fi)

        def gather(src_d, src_b, dst_d, dst_b):
            sc.op(POOL, I("collective_compute",
                          "AllGather", ALU.bypass, replica_groups=[[0, 1], [2, 3], [4, 5], [6, 7]],
                          ins=[src_d.ap().opt()], outs=[dst_d.ap().opt()]),
                  reads=[src_b], writes=[dst_b], kind="cc", dsem=dst_b.dsem)

        def phase_A_back(t4, stq=None):
            norm_back(t4, hTt[1], b_hTt[1], 11)
            sc.op(stq or SP, I("dma_start", out=hb_d[t4][:, :].rearrange("(k p) t -> p k t", p=128), in_=hTt[1][:, :, :]),
                  reads=[b_hTt[1]], writes=[hb_b[t4]], kind="dma",
                  dsem=(b_swst.dsem if stq is POOL else b_hTt[1].dsem))
            gather(hb_d[t4], hb_b[t4], hg_d[t4], hg_b[t4])

        def phase_A_tile(t4, stq=None):
            norm_front(t4, hTt[0], b_hTt[0], 7, 11)
            phase_A_back(t4, stq)

        def load_y(t4, extra=None):
            ys = t4 % 2
            yT = yTt[ys]
            t0 = t4 * 512
            plan = [(0, 0, 0, 2), (0, 1, 2, 2), (1, 0, 4, 1), (1, 1, 6, 1), (2 + t4, 0, 5, 1), (2 + t4, 1, 7, 1)]
            for pi, (part, r, c0, nch) in enumerate(plan):
                def ld(e, part=part, r=r, c0=c0, nch=nch, yT=yT, t0=t0):
                    if "half_sp" not in half_cache:
                        half_cache["half_sp"] = e.snap(e.partition_id() % 2, min_val=0, max_val=1)
                    half = half_cache["half_sp"]
                    if part < 2:
                        ygv = yg_d[part].ap().rearrange("(r c p) (hh t) -> r hh p c t", r=2, c=nch, p=128, hh=2)
                        src = ygv[r, bass.ds(half, 1), :, :, t0:t0 + 512]
                    else:
                        ygv = yg_d[part].ap().rearrange("(r c p) (hh t) -> r hh p c t", r=2, c=1, p=128, hh=2)
                        src = ygv[r, bass.ds(half, 1), :, :, :]
                    return e.dma_start(out=yT[:, c0:c0 + nch, :], in_=src)
                sc.op(SP, ld, reads=[yg_b[part]], writes=[b_yT[ys]], kind="dma", dsem=b_yT[ys].dsem,
                      merge_w=(pi > 0), extra=extra)


        def phase_B(layer):
            for cc in range(2):
                sc.op(POOL, I("memset", Fp[6 + cc][:, 0:2], 0.0), writes=[b_F[6 + cc]])
            rot = [0]

            def load_hT(tt):
                hs_, half_ = tt % 2, tt // 4
                sc.op(SP, I("dma_start", out=hTt[hs_][:, :, :],
                            in_=hg_d[tt % 4][half_ * D:(half_ + 1) * D, :].rearrange("(k p) t -> p k t", p=128)),
                      reads=[hg_b[tt % 4]], writes=[b_hTt[hs_]], kind="dma", dsem=b_hTt[hs_].dsem)

            for tt in range(8):
                hs = tt % 2
                hT = hTt[hs]
                half, tl = tt // 4, (tt % 4) * 512
                tok0 = tt * 512
                if tt == 0:
                    load_hT(0)

                def proj(psi, col0):
                    for k in range(8):
                        sc.op(PE, I("matmul", P[psi][:, :], lhsT=win[:, k, col0:col0 + 128],
                                    rhs=hT[:, k, :], start=(k == 0), stop=(k == 7)),
                              reads=[b_hTt[hs], b_wlo, b_whi], writes=[b_P[psi]])

                def nextrot():
                    psi = 5 + rot[0] % 3
                    rot[0] += 1
                    return psi

                def conv_front(cc):
                    for si in range(4):
                        proj(si, si * 256 + cc * 128)
                    u, ub = Fp[6 + cc], b_F[6 + cc]
                    w_ = lambda j: vcol(layer, V_CW + cc * 3 + j)
                    sc.op(ACT, I("activation", out=Fp[0][:, 0:512], in_=P[1][:, :], func=AF.Copy),
                          reads=[b_P[1]], writes=[b_F[0]])
                    sigmoid_act(P[3], b_P[3], Fp[5], b_F[5])
                    sc.op(DVE, I("tensor_tensor", out=u[:, 2:514], in0=Fp[0][:, 0:512], in1=P[2][:, :], op=ALU.mult),
                          reads=[b_F[0], b_P[2]], writes=[ub], merge_w=True)
                    sc.op(DVE, I("tensor_scalar", out=Fp[1][:, 0:512], in0=u[:, 2:514], scalar1=w_(2),
                                 scalar2=vcol(layer, V_CB + cc), op0=ALU.mult, op1=ALU.add),
                          reads=[ub, b_vec], writes=[b_F[1]])
                    sc.op(DVE, I("scalar_tensor_tensor", out=Fp[2][:, 0:512], in0=u[:, 1:513], scalar=w_(1),
                                 in1=Fp[1][:, 0:512], op0=ALU.mult, op1=ALU.add),
                          reads=[ub, b_vec, b_F[1]], writes=[b_F[2]])
                    sc.op(DVE, I("scalar_tensor_tensor", out=Fp[1][:, 0:512], in0=u[:, 0:512], scalar=w_(0),
                                 in1=Fp[2][:, 0:512], op0=ALU.mult, op1=ALU.add),
                          reads=[ub, b_vec, b_F[2]], writes=[b_F[1]])
                    sc.op(POOL, I("tensor_copy", out=u[:, 0:2], in_=u[:, 512:514]), reads=[ub], writes=[ub])
                    sc.op(DVE, I("tensor_tensor", out=Fp[3][:, 0:512], in0=Fp[1][:, 0:512], in1=P[0][:, :],
                                 op=ALU.mult), reads=[b_F[1], b_P[0]], writes=[b_F[3]])
                    sc.op(POOL, I("tensor_tensor", out=Bp[0][:, :], in0=Fp[3][:, 0:512], in1=Fp[3][:, 0:512],
                                  op=ALU.mult), reads=[b_F[3]], writes=[b_B[0]])
                    sc.op(DVE, I("tensor_tensor", out=Fp[2][:, 0:512], in0=P[3][:, :], in1=Fp[5][:, 0:512],
                                 op=ALU.mult), reads=[b_P[3], b_F[5]], writes=[b_F[2]])

                def conv_back(cc):
                    sc.op(PE, I("matmul", P[4][:, :], lhsT=gsum, rhs=Bp[0][:, :], start=True, stop=True),
                          reads=[b_B[0], b_cst], writes=[b_P[4]])
                    rstd_from(b_P[4], P[4], b_F[4], Fp[4])
                    sc.op(DVE, I("tensor_tensor", out=Fp[3][:, 0:512], in0=Fp[3][:, 0:512], in1=Fp[4][:, 0:512],
                                 op=ALU.mult), reads=[b_F[3], b_F[4]], writes=[b_F[3]])
                    ob = 1 + cc
                    sc.op(DVE, I("scalar_tensor_tensor", out=Bp[ob][:, :], in0=Fp[3][:, 0:512],
                                 scalar=vcol(layer, V_BGC + cc), in1=Fp[2][:, 0:512], op0=ALU.mult, op1=ALU.mult),
                          reads=[b_F[3], b_F[2], b_vec], writes=[b_B[ob]])
                    sc.op(SP, I("dma_start", out=yb_d[0][cc * 128:(cc + 1) * 128, tok0:tok0 + 512], in_=Bp[ob][:, :]),
                          reads=[b_B[ob]], writes=[yb_b[0]], kind="dma", dsem=b_B[ob].dsem, merge_w=True)

                def qk_part():
                    for pair in range(2):
                        psi = nextrot()
                        proj(psi, 1024 + pair * 128)
                        sc.op(ACT, I("activation", out=qT[:, pair, tok0:tok0 + 512], in_=P[psi][:, :], func=AF.Copy,
                                     scale=0.125), reads=[b_P[psi]], writes=[b_q], merge_w=True)
                        psi = nextrot()
                        proj(psi, 1280 + pair * 128)
                        sc.op(ACT, I("activation", out=kT[:, pair, tok0:tok0 + 512], in_=P[psi][:, :], func=AF.Copy),
                              reads=[b_P[psi]], writes=[b_k], merge_w=True)

                def azv_part():
                    for bp in range(2):
                        psi = nextrot()
                        for bl in range(2):
                            blk = bp * 2 + bl
                            for k in range(8):
                                sc.op(PE, I("matmul", P[psi][:, bl * 256:(bl + 1) * 256],
                                            lhsT=hT[:, k, blk * 128:(blk + 1) * 128],
                                            rhs=win[:, k, 1536:1792], start=(k == 0), stop=(k == 7)),
                                      reads=[b_hTt[hs], b_wlo, b_whi], writes=[b_P[psi]])
                        sc.op(ACT, I("activation", out=V[:, tt * 4 + bp * 2: tt * 4 + bp * 2 + 2, :],
                                     in_=P[psi][:, :].rearrange("p (n c) -> p n c", c=256), func=AF.Copy),
                              reads=[b_P[psi]], writes=[b_v], merge_w=True)
                    for pair in range(2):
                        psi = nextrot()
                        proj(psi, 1792 + pair * 128)
                        gb = 3 + pair
                        fe, fe_b = Fp[8 + pair], b_F[8 + pair]
                        sigmoid_act(P[psi], b_P[psi], fe, fe_b)
                        sc.op(DVE, I("tensor_tensor", out=Bp[gb][:, :], in0=P[psi][:, :], in1=fe[:, 0:512],
                                     op=ALU.mult), reads=[b_P[psi], fe_b], writes=[b_B[gb]])
                        sc.op(SP, I("dma_start", out=gz_d[pair * 128:(pair + 1) * 128, tok0:tok0 + 512],
                                    in_=Bp[gb][:, :]),
                              reads=[b_B[gb]], writes=[gz_b], kind="dma", dsem=b_B[gb].dsem, merge_w=True)

                conv_front(0)
                if tt + 1 < 8:
                    load_hT(tt + 1)
                qk_part()
                conv_back(0)
                conv_front(1)
                azv_part()
                conv_back(1)

        def phase_C(layer, wd_list):
            for i in range(8):
                merge(b_Bx[i].r, b_hTt[0].r)
                merge(b_Bx[i].r, b_hTt[0].w)
                merge(b_Bx[i].w, b_hTt[0].w)
            streams = []
            for pair in range(2):
                for qi in range(8):
                    nkb = 4 * (qi + 1)
                    streams.append([(pair, qi, kb, i == 0, kb == 0) for i, kb in enumerate(range(nkb - 1, -1, -1))])
            sblocks = []
            pre = [False] * len(streams)
            for i, st in enumerate(streams):
                pend = st[4:] if pre[i] else st
                nfull = len(st) - 4
                if i + 1 < len(streams) and nfull >= 4 and len(streams[i + 1]) > 4:
                    sblocks += pend[:-4]
                    for j in range(4):
                        sblocks.append(pend[len(pend) - 4 + j])
                        sblocks.append(streams[i + 1][j])
                    pre[i + 1] = True
                else:
                    sblocks += pend
            ns = len(sblocks)
            assert ns == sum(len(st) for st in streams)

            def cbp(s):
                return 2 if (sblocks[s][0] * 8 + sblocks[s][1]) % 2 == 0 else 1
            OB = [6, 7]
            eF = [[0, 2, 4]]
            wF = [[6, 8]]
            F_O, F_R = 10, 11
            Lb = [[(Bp[0], b_B[0]), (Bp[1], b_B[1])], [(Bp[2], b_B[2]), (Bp[3], b_B[3])]]
            Ab = [[(Bp[4], b_B[4]), (Bp[5], b_B[5])], [(Bp[6], b_B[6]), (Bp[7], b_B[7])]]
            SQ = (Bx[0], b_Bx[0])
            GT = [(Bx[1], b_Bx[1]), (Bx[2], b_Bx[2])]
            YA = [(Bx[3], b_Bx[3]), (Bx[4], b_Bx[4])]

            def c0of(s):
                pair, qi, kb, first, last = sblocks[s]
                r = kb - 4 * qi
                return r * 128 if r > 0 else 0

            def qk(s, hh):
                pair, qi, kb, first, last = sblocks[s]
                z = hh
                c0 = c0of(s)
                pr = slice(hh * 64, (hh + 1) * 64)
                sc.op(PE, I("matmul", P[z][:, c0:512], lhsT=kT[pr, pair, kb * 128:(kb + 1) * 128],
                            rhs=qT[pr, pair, qi * 512 + c0:(qi + 1) * 512], start=True, stop=True),
                      reads=[b_q, b_k], writes=[b_P[z]])

            def exp_z(s):
                c0 = c0of(s)
                f = eF[0][s % 3]
                sc.op(ACT, I("activation", out=Fpair(f)[:, :, c0:512], in_=Ppair(0)[:, :, c0:512], func=AF.Exp),
                      reads=[b_P[0], b_P[1]], writes=[b_F[f], b_F[f + 1]])
                pair_, qi_, kb_ = sblocks[s][0], sblocks[s][1], sblocks[s][2]
                if kb_ >= 4 * qi_:
                    for hh in range(2):
                        sc.op(DVE, I("tensor_tensor", out=Fp[f + hh][:, c0:c0 + 128], in0=Fp[f + hh][:, c0:c0 + 128],
                                     in1=mask01, op=ALU.mult),
                              reads=[b_F[f + hh], b_cst], writes=[b_F[f + hh]])

            def ln_l(s):
                c0 = c0of(s)
                f = eF[0][s % 3]
                li = 2 * (s % 2)
                sc.op(ACT, I("activation", out=bfa[:, 16 + li:16 + li + 2, c0:512], in_=Fpair(f)[:, :, c0:512],
                             func=AF.Ln, bias=1.0),
                      reads=[b_F[f], b_F[f + 1]], writes=[b_B[li], b_B[li + 1]])

            def tri(s, hh):
                first = sblocks[s][3]
                c0 = c0of(s)
                li = 2 * (s % 2) + hh
                cb = 2 * cbp(s) + hh
                sc.op(PE, I("matmul", P[cb][:, c0:512], lhsT=triN, rhs=Bp[li][:, c0:512], start=first, stop=True,
                            skip_group_check=True),
                      reads=[b_B[li], b_cst], writes=[b_P[cb]])

            def tric(s, hh):
                c0 = c0of(s)
                li = 2 * (s % 2) + hh
                cb = 2 * cbp(s) + hh
                sc.op(PE, I("matmul", P[cb][:, c0:512], lhsT=triC, rhs=Bp[li][:, c0:512], start=False, stop=True,
                            skip_group_check=True),
                      reads=[b_B[li], b_cst], writes=[b_P[cb]])

            def w_exp(s):
                c0 = c0of(s)
                f = wF[0][s % 2]
                cp_ = cbp(s)
                sc.op(ACT, I("activation", out=Fpair(f)[:, :, c0:512], in_=Ppair(cp_)[:, :, c0:512], func=AF.Exp),
                      reads=[b_P[2 * cp_], b_P[2 * cp_ + 1]], writes=[b_F[f], b_F[f + 1]])

            def mul_a(s):
                c0 = c0of(s)
                fe, fw = eF[0][s % 3], wF[0][s % 2]
                ai = 4 + 2 * (s % 2)
                sc.op(DVE, I("tensor_tensor", out=bfa[:, 16 + ai:16 + ai + 2, c0:512], in0=Fpair(fe)[:, :, c0:512],
                             in1=Fpair(fw)[:, :, c0:512], op=ALU.mult),
                      reads=[b_F[fe], b_F[fe + 1], b_F[fw], b_F[fw + 1]], writes=[b_B[ai], b_B[ai + 1]])

            def av(s, hh):
                pair, qi, kb, first, last = sblocks[s]
                c0 = c0of(s)
                ai = 4 + 2 * (s % 2) + hh
                o = OB[(pair * 8 + qi) % 2]
                h4 = pair * 2 + hh
                sc.op(PE, I("matmul", P[o][hh * 64:(hh + 1) * 64, c0:512], lhsT=V[:, kb, h4 * 64:(h4 + 1) * 64],
                            rhs=Bp[ai][:, c0:512], start=first, stop=last, skip_group_check=True),
                      reads=[b_v, b_B[ai]], writes=[b_P[o]])

            def epilogue1(pair, qi):
                o = OB[(pair * 8 + qi) % 2]
                tok0 = qi * 512
                gt, gb = GT[(pair * 8 + qi) % 2]
                sc.op(SP, I("dma_start", out=gt[:, :], in_=gz_d[pair * 128:(pair + 1) * 128, tok0:tok0 + 512]),
                      reads=[gz_b], writes=[gb], kind="dma", dsem=gb.dsem)
                sc.op(DVE, I("tensor_copy", out=Fp[F_O][:, 0:512], in_=P[o][:, :]),
                      reads=[b_P[o]], writes=[b_F[F_O]])
                sc.op(DVE, I("tensor_tensor", out=SQ[0][:, :], in0=Fp[F_O][:, 0:512], in1=Fp[F_O][:, 0:512],
                             op=ALU.mult), reads=[b_F[F_O]], writes=[SQ[1]])

            def epilogue1b(pair, qi):
                o = OB[(pair * 8 + qi) % 2]
                sc.op(PE, I("matmul", P[o][:, :], lhsT=gsum, rhs=SQ[0][:, :], start=True, stop=True),
                      reads=[SQ[1], b_cst], writes=[b_P[o]])

            def epilogue2(pair, qi):
                o = OB[(pair * 8 + qi) % 2]
                tok0 = qi * 512
                gt, gb = GT[(pair * 8 + qi) % 2]
                yt, yb_ = YA[(pair * 8 + qi) % 2]
                rstd_from(b_P[o], P[o], b_F[F_R], Fp[F_R])
                sc.op(DVE, I("tensor_tensor", out=Fp[F_O][:, 0:512], in0=Fp[F_O][:, 0:512], in1=Fp[F_R][:, 0:512],
                             op=ALU.mult), reads=[b_F[F_O], b_F[F_R]], writes=[b_F[F_O]])
                sc.op(DVE, I("scalar_tensor_tensor", out=yt[:, :], in0=Fp[F_O][:, 0:512],
                             scalar=vcol(layer, V_BGA + pair), in1=gt[:, :], op0=ALU.mult, op1=ALU.mult),
                      reads=[b_F[F_O], gb, b_vec], writes=[yb_])
                if pair == 0:
                    part, dst = 1, yb_d[1][:, tok0:tok0 + 512]
                else:
                    part = 2 + qi % 4
                    dst = yb_d[part][:, (qi // 4) * 512:(qi // 4 + 1) * 512]
                sc.op(SP, I("dma_start", out=dst, in_=yt[:, :]),
                      reads=[yb_], writes=[yb_b[part]], kind="dma", dsem=yb_.dsem, merge_w=True)
                if pair == 0 and qi == 7:
                    gather(yb_d[1], yb_b[1], yg_d[1], yg_b[1])
                if pair == 1 and qi >= 4:
                    gather(yb_d[part], yb_b[part], yg_d[part], yg_b[part])
                if pair == 1 and qi == 5:
                    load_y(0, extra=dict(b_q.r))

            wd_it = iter(wd_list)
            pending = []
            for hh in range(2):
                qk(0, hh)
            for s in range(ns + 1):
                if s < ns:
                    exp_z(s)
                if s + 1 < ns:
                    for hh in range(2):
                        qk(s + 1, hh)
                if s >= 1:
                    w_exp(s - 1)
                if s < ns:
                    ln_l(s)
                if s >= 1 and not sblocks[s - 1][4]:
                    for hh in range(2):
                        tric(s - 1, hh)
                if s < ns:
                    for hh in range(2):
                        tri(s, hh)
                if s >= 1:
                    mul_a(s - 1)
                    for hh in range(2):
                        av(s - 1, hh)
                    if sblocks[s - 1][4]:
                        epilogue1(sblocks[s - 1][0], sblocks[s - 1][1])
                        pending.append((s + 1, 0, sblocks[s - 1][0], sblocks[s - 1][1]))
                        pending.append((s + 3, 1, sblocks[s - 1][0], sblocks[s - 1][1]))
                        pending.sort()
                while pending and pending[0][0] <= s:
                    _, st_, p_, q_ = pending.pop(0)
                    (epilogue1b if st_ == 0 else epilogue2)(p_, q_)
                if s % 6 == 3:
                    nxt = next(wd_it, None)
                    if nxt is not None:
                        w_piece(nxt[0], nxt[1], nxt[2], DVE)
            for _, st_, p_, q_ in sorted(pending):
                (epilogue1b if st_ == 0 else epilogue2)(p_, q_)
            for nxt in wd_it:
                w_piece(nxt[0], nxt[1], nxt[2], DVE)

        def phase_D(layer):
            last_layer = layer == DEPTH - 1
            pTbs = [big[:, 16384 + i * 1024: 16384 + (i + 1) * 1024].rearrange("p (c t) -> p c t", c=2)
                    for i in range(2)]
            b_pT2 = [b_pTb, b_v]

            def load_p(t4):
                t0 = t4 * 512
                for j in range(2):
                    sc.op(SP, I("dma_start", out=Fp[j][:, 0:512], in_=pT_d[layer, :, j, t0:t0 + 512]),
                          writes=[b_F[j]], kind="dma", dsem=b_F[j].dsem)
                    sc.op(ACT, I("activation", out=pTbs[t4 % 2][:, j, :], in_=Fp[j][:, 0:512], func=AF.Copy),
                          reads=[b_F[j]], writes=[b_pT2[t4 % 2]], merge_w=(j > 0))

            def outproj(t4, n0, n1):
                t0 = t4 * 512
                ys = t4 % 2
                yT = yTt[ys]
                for n in range(n0, n1):
                    psi = n % 2
                    for c in range(8):
                        sc.op(PE, I("matmul", P[psi][:, :], lhsT=wout[:, c, n * 128:(n + 1) * 128], rhs=yT[:, c, :],
                                    start=(c == 0), stop=(c == 7)), reads=[b_yT[ys], b_wlo], writes=[b_P[psi]])
                    sc.op(DVE, I("tensor_tensor", out=xT[:, n, t0:t0 + 512], in0=xT[:, n, t0:t0 + 512],
                                 in1=P[psi][:, :], op=ALU.add),
                          reads=[b_P[psi], b_xT[t4]], writes=[b_xT[t4]])

            hps = [hp, big[:, 12288:16384].rearrange("p (c t) -> p c t", c=8)]
            b_hps = [b_hp, b_hp2]

            def gate(t4, n0=0, n1=8):
                t0 = t4 * 512
                pTb_ = pTbs[t4 % 2]
                hp_, b_hp_ = hps[t4 % 2], b_hps[t4 % 2]
                for n in range(n0, n1):
                    pg, pe_ = 2 + (n % 2) * 2, 3 + (n % 2) * 2
                    for k in range(8):
                        sc.op(PE, I("matmul", P[pg][:, :], lhsT=wpg[:, k, n * 128:(n + 1) * 128], rhs=hp_[:, k, :],
                                    start=(k == 0), stop=(k == 7)), reads=[b_hp_, b_whi], writes=[b_P[pg]])
                    for j in range(2):
                        sc.op(PE, I("matmul", P[pe_][:, :], lhsT=wpe[:, j, n * 128:(n + 1) * 128], rhs=pTb_[:, j, :],
                                    start=(j == 0), stop=(j == 1)), reads=[b_pT2[t4 % 2], b_whi], writes=[b_P[pe_]])
                    f = 2 + n % 2
                    sigmoid_act(P[pg], b_P[pg], Fp[f], b_F[f], nbias=nbpg[:, layer * 8 + n: layer * 8 + n + 1])
                    sc.op(DVE, I("tensor_tensor", out=Fp[f][:, 0:512], in0=Fp[f][:, 0:512], in1=P[pe_][:, :],
                                 op=ALU.mult), reads=[b_F[f], b_P[pe_]], writes=[b_F[f]])
                    sc.op(DVE, I("tensor_tensor", out=xT[:, n, t0:t0 + 512], in0=xT[:, n, t0:t0 + 512],
                                 in1=Fp[f][:, 0:512], op=ALU.add),
                          reads=[b_F[f], b_xT[t4]], writes=[b_xT[t4]])

            def final_back(t4):
                t0 = t4 * 512
                for k in range(8):
                    sc.op(DVE, I("scalar_tensor_tensor", out=xT[:, k, t0:t0 + 512], in0=xT[:, k, t0:t0 + 512],
                                 scalar=fgcol(k), in1=Fp[11][:, 0:512], op0=ALU.mult, op1=ALU.mult),
                          reads=[b_xT[t4], b_F[11], b_vec], writes=[b_xT[t4]])
                for hf in range(2):
                    sc.op(SP, I("dma_start", out=outT_d[:, hf * 4:(hf + 1) * 4, t0:t0 + 512],
                                in_=xT[:, hf * 4:(hf + 1) * 4, t0:t0 + 512]),
                          reads=[b_xT[t4]], writes=[out_b], kind="dma", dsem=out_b.dsem, merge_w=True)

            ew = {"on": False, "i": 0}
            late_casts = []
            nxt_pieces = win_pieces(layer + 1) if not last_layer else []

            def early_w(n, fslots=None, limit=16, defer=None):
                if not ew["on"]:
                    return
                for _ in range(n):
                    if ew["i"] < min(limit, len(nxt_pieces)):
                        src, off, scol = nxt_pieces[ew["i"]]
                        w_piece(src, off, scol, ACT if ew["i"] % 2 == 0 else DVE, fslots=fslots, defer=defer)
                        ew["i"] += 1

            load_p(0)
            load_y(1)
            outproj(0, 0, 8)
            load_p(1)
            load_y(2)
            norm_front(0, hTt[0], b_hTt[0], 7, 7)
            norm_back(0, hps[0], b_hps[0], 7)
            outproj(1, 0, 8)
            load_y(3)
            for t4 in range(4):
                nt = t4 + 1 < 4
                if nt:
                    norm_sq(t4 + 1, hTt[0], b_hTt[0])
                gate(t4, 0, 2)
                if nt:
                    norm_ms(hTt[0], b_hTt[0], 7)
                    rstd_from(b_P[7], P[7], b_F[7], Fp[7])
                gate(t4, 2, 4)
                early_w(4)
                if nt:
                    norm_back(t4 + 1, hps[(t4 + 1) % 2], b_hps[(t4 + 1) % 2], 7)
                gate(t4, 4, 8)
                early_w(4)
                if t4 == 3:
                    early_w(15, fslots=[4, 5, 6, 8, 9, 10, 7, 0, 1, 2, 3], limit=32, defer=late_casts)
                norm_sq(t4, hTt[0], b_hTt[0])
                if t4 + 2 < 4:
                    load_p(t4 + 2)
                    outproj(t4 + 2, 0, 2)
                norm_ms(hTt[0], b_hTt[0], 6)
                rstd_from(b_P[6], P[6], b_F[11], Fp[11])
                if t4 + 2 < 4:
                    outproj(t4 + 2, 2, 8)
                    if t4 + 2 == 3:
                        ew["on"] = True
                if not last_layer:
                    phase_A_back(t4)
                else:
                    final_back(t4)
            for c_ in late_casts:
                c_()
            return ew["i"]

        early_done = [0]

        class _Stop(Exception):
            pass
        nst = [0]

        def stage():
            nst[0] += 1
            if stop_after is not None and nst[0] > stop_after:
                raise _Stop()
        try:
            for t4 in range(4):
                phase_A_tile(t4, stq=POOL)
                load_win(0, t4 * 8, (t4 + 1) * 8)
            for layer in range(DEPTH):
                stage()
                xch = [b_.dsem for b_ in yg_b + hg_b]
                if layer > 0:
                    sc.barrier(exclude=xch)
                    load_win(layer, early_done[0], 32)
                stage()
                phase_B(layer)
                gather(yb_d[0], yb_b[0], yg_d[0], yg_b[0])
                stage()
                phase_C(layer, wd_pieces(layer))
                stage()
                sc.barrier(exclude=xch)
                stage()
                early_done[0] = phase_D(layer)
        except _Stop:
            pass
        if debug:
            sc.barrier()
            d1 = sc.buf("dbg1")
            for i in range(4):
                sc.op(SP, I("dma_start", out=dbg["hg"][i][:, :], in_=hg_d[i][:, :]), writes=[d1], kind="dma",
                      dsem=d1.dsem, merge_w=True)
            for i in range(NY):
                sc.op(SP, I("dma_start", out=dbg["yg"][i][:, :], in_=yg_d[i][:, :]), writes=[d1], kind="dma",
                      dsem=d1.dsem, merge_w=True)
        sc.barrier()

        for s_ in sc.sems():
            s_.h = es.enter_context(nc.semaphore(s_.name))
        with nc.Block() as block:
            @block.tensor
            def _(eng):
                replay(PE, eng)

            @block.scalar
            def _(eng):
                replay(ACT, eng)

            @block.vector
            def _(eng):
                replay(DVE, eng)

            @block.gpsimd
            def _(eng):
                replay(POOL, eng)

            @block.sync
            def _(eng):
                replay(SP, eng)
    return nc


def _chunk_rows(a):
    k = a.shape[0] // 128
    return np.ascontiguousarray(a.reshape(k, 128, a.shape[1]).transpose(1, 0, 2))


def _consts():
    j = np.arange(128)[:, None]
    s = np.arange(128)[None, :]
    triN = np.where(j >= s, -1.0, 0.0)
    triC = np.where(j < s, -1.0, 0.0)
    ident = np.eye(128)
    gsum = np.where((j // 64) == (s // 64), 1.0 / 64, 0.0)
    onesms = np.full((128, 128), 1.0 / 1024)
    mask01 = np.where(j < s, 1.0, 0.0)
    return np.ascontiguousarray(np.concatenate([triN, triC, ident, gsum, onesms, mask01], axis=1).astype(np.float32))


def prep_inputs(x, p, norm_g, w_in, conv_w, conv_b, branch_g, w_out, ple_norm_g, w_pg, b_pg, w_pe, final_g):
    f = lambda a: np.asarray(a, dtype=np.float32)
    x, p, norm_g, w_in, conv_w, conv_b, branch_g, w_out, ple_norm_g, w_pg, b_pg, w_pe, final_g = map(
        f, (x, p, norm_g, w_in, conv_w, conv_b, branch_g, w_out, ple_norm_g, w_pg, b_pg, w_pe, final_g))
    cst = _consts()
    per_g = []
    for g in range(2):
        cols = np.concatenate([np.arange(seg * 512 + g * 256, seg * 512 + (g + 1) * 256) for seg in range(8)])
        win = np.stack([_chunk_rows(w_in[i][:, cols]) for i in range(DEPTH)])
        vecs = []
        for i in range(DEPTH):
            cw = conv_w[i][:, g * 256:(g + 1) * 256].reshape(3, 2, 128).transpose(2, 1, 0).reshape(128, 6)
            v = np.concatenate([
                norm_g[i].reshape(8, 128).T, ple_norm_g[i].reshape(8, 128).T, b_pg[i].reshape(8, 128).T,
                cw, conv_b[i][g * 256:(g + 1) * 256].reshape(2, 128).T,
                branch_g[i][g * 256:(g + 1) * 256].reshape(2, 128).T,
                branch_g[i][512 + g * 256: 512 + (g + 1) * 256].reshape(2, 128).T], axis=1)
            vecs.append(v)
        vecs.append(final_g.reshape(8, 128).T)
        per_g.append((win, np.ascontiguousarray(np.concatenate(vecs, axis=1).astype(np.float32))))
    wout = np.stack([_chunk_rows(w_out[i]) for i in range(DEPTH)])
    wpg = np.stack([_chunk_rows(w_pg[i]) for i in range(DEPTH)])
    wpe = np.stack([_chunk_rows(w_pe[i]) for i in range(DEPTH)])
    maps = []
    for c in range(NCORES):
        b, g = c // 2, c % 2
        xT = _chunk_rows(np.ascontiguousarray(x[b, g * HALF:(g + 1) * HALF, :].T))
        pT = np.stack([_chunk_rows(np.ascontiguousarray(p[i, b, g * HALF:(g + 1) * HALF, :].T)) for i in range(DEPTH)])
        maps.append({"xT": xT, "pT": pT, "win": per_g[g][0], "wout": wout, "wpg": wpg, "wpe": wpe,
                     "vec": per_g[g][1], "cst": cst})
    return maps


def assemble(results):
    out = np.empty((4, S, D), np.float32)
    for c in range(NCORES):
        b, g = c // 2, c % 2
        oT = np.asarray(results[c]["outT"])
        out[b, g * HALF:(g + 1) * HALF, :] = oT.transpose(2, 1, 0).reshape(HALF, D)
    return out


_NC_CACHE = {}


def kernel(**inputs):
    maps = prep_inputs(**inputs)
    if "nc" not in _NC_CACHE:
        _NC_CACHE["nc"] = build_program()
    res = run_bass_kernel_spmd(_NC_CACHE["nc"], maps, core_ids=list(range(NCORES)))
    return assemble(res.results)
```

```python
import contextlib
import numpy as np
import concourse.bass as bass
import concourse.mybir as mybir
from concourse.bass_utils import run_bass_kernel_spmd

F32 = mybir.dt.float32
BF16 = mybir.dt.bfloat16
AF = mybir.ActivationFunctionType
ALU = mybir.AluOpType

DEPTH = 2
S = 4096
HALF = 2048
D = 1024
NEG = -30000.0
EPS = 1e-6
NCORES = 8


class Sem:
    def __init__(self, name):
        self.name = name
        self.h = None
        self.val = 0


class Q:
    def __init__(self, name):
        self.name = name
        self.ops = []
        self.sem = Sem("q" + name)


class Buf:
    def __init__(self, name):
        self.name = name
        self.w = {}
        self.r = {}
        self.dsem = Sem("d" + name)


def merge(a, b):
    for k, v in b.items():
        if a.get(k, 0) < v:
            a[k] = v


class Sched:
    def __init__(self):
        self.pe = Q("pe")
        self.act = Q("act")
        self.dve = Q("dve")
        self.pool = Q("pool")
        self.sp = Q("sp")
        self.queues = [self.pe, self.act, self.dve, self.pool, self.sp]
        self.bufs = []
        self.extra_sems = []

    def buf(self, name):
        b = Buf(name)
        self.bufs.append(b)
        return b

    def op(self, q, fn, reads=(), writes=(), kind="c", dsem=None, merge_w=False, extra=None, no_waw=False):
        waits = {}
        for b in reads:
            merge(waits, b.w)
        for b in writes:
            merge(waits, b.r)
            if not no_waw:
                merge(waits, b.w)
        if extra:
            merge(waits, extra)
        if q is self.pe:
            waits.pop(self.pe.sem, None)
        if kind == "c":
            q.sem.val += 1
            tok = {q.sem: q.sem.val}
            sig = (q.sem, 1)
        elif kind == "dma":
            dsem.val += 16
            tok = {dsem: dsem.val}
            sig = (dsem, 16)
        else:
            dsem.val += 1
            tok = {dsem: dsem.val}
            sig = (dsem, 1)
        for b in writes:
            if merge_w:
                merge(b.w, tok)
            else:
                b.w = dict(tok)
            if not no_waw:
                b.r = {}
        for b in reads:
            merge(b.r, tok)
        q.ops.append((fn, waits, sig))
        return tok

    def all_tokens(self):
        t = {}
        for q in self.queues:
            if q.sem.val:
                t[q.sem] = q.sem.val
        for b in self.bufs:
            if b.dsem.val:
                t[b.dsem] = b.dsem.val
        for s in self.extra_sems:
            if s.val:
                t[s] = s.val
        return t

    def barrier(self, extra=None, exclude=()):
        t = self.all_tokens()
        for sem in exclude:
            t.pop(sem, None)
        if extra:
            merge(t, extra)
        for q in self.queues:
            w = dict(t)
            q.ops.append((None, w, None))

    def sems(self):
        out = [q.sem for q in self.queues]
        out += [b.dsem for b in self.bufs if b.dsem.val]
        out += [s for s in self.extra_sems]
        return out


def I(name, *args, **kw):
    return (name, args, kw)


def replay(q, eng):
    seen = {}
    for fn, waits, sig in q.ops:
        for sem, val in waits.items():
            if seen.get(sem, 0) < val:
                eng.wait_ge(sem.h, val)
                seen[sem] = val
        if fn is None:
            continue
        if callable(fn):
            ins = fn(eng)
        else:
            ins = getattr(eng, fn[0])(*fn[1], **fn[2])
        if sig is not None:
            ins.then_inc(sig[0].h, sig[1])


def build_program(debug=False, stop_after=None):
    nc = bass.Bass("TRN2", target_bir_lowering=False)
    sc = Sched()
    PE, ACT, DVE, POOL, SP = sc.pe, sc.act, sc.dve, sc.pool, sc.sp

    def ext_in(name, shape, dt=F32):
        return nc.dram_tensor(name, shape, dt, kind="ExternalInput")

    xT_d = ext_in("xT", [128, 8, HALF])
    pT_d = ext_in("pT", [DEPTH, 128, 2, HALF])
    win_d = ext_in("win", [DEPTH, 128, 8, 2048])
    wout_d = ext_in("wout", [DEPTH, 128, 8, 1024])
    wpg_d = ext_in("wpg", [DEPTH, 128, 8, 1024])
    wpe_d = ext_in("wpe", [DEPTH, 128, 2, 1024])
    NV = 8 + 8 + 8 + 6 + 2 + 2 + 2
    vec_d = ext_in("vec", [128, DEPTH * NV + 8])
    cst_d = ext_in("cst", [128, 6 * 128])
    outT_d = nc.dram_tensor("outT", [128, 8, HALF], F32, kind="ExternalOutput")

    hb_d = [nc.dram_tensor(f"hb{i}", [D, 512], BF16) for i in range(4)]
    hg_d = [nc.dram_tensor(f"hg{i}", [2 * D, 512], BF16) for i in range(4)]
    YSH = [(256, S), (128, S)] + [(128, 1024)] * 4
    NY = len(YSH)
    yb_d = [nc.dram_tensor(f"yb{i}", [YSH[i][0], YSH[i][1]], BF16) for i in range(NY)]
    yg_d = [nc.dram_tensor(f"yg{i}", [2 * YSH[i][0], YSH[i][1]], BF16) for i in range(NY)]
    gz_d = nc.dram_tensor("gz", [256, S], BF16)
    dbg = {}
    if debug:
        dbg["hg"] = [nc.dram_tensor(f"dbg_hg{i}", [2 * D, 512], BF16, kind="ExternalOutput") for i in range(4)]
        dbg["yg"] = [nc.dram_tensor(f"dbg_yg{i}", [2 * YSH[i][0], YSH[i][1]], BF16, kind="ExternalOutput")
                     for i in range(NY)]

    hb_b = [sc.buf(f"hb{i}") for i in range(4)]
    hg_b = [sc.buf(f"hg{i}") for i in range(4)]
    yb_b = [sc.buf(f"yb{i}") for i in range(NY)]
    yg_b = [sc.buf(f"yg{i}") for i in range(NY)]
    gz_b = sc.buf("gz")
    out_b = sc.buf("outd")
    cc_sem = Sem("cc")
    sc.extra_sems.append(cc_sem)

    with contextlib.ExitStack() as es:
        def sb(name, shape, dt):
            return es.enter_context(nc.sbuf_tensor(name, shape, dt))

        def ps(name):
            return es.enter_context(nc.psum_tensor(name, [128, 512], F32))

        xT = sb("xT_s", [128, 8, HALF], F32)
        wbuf = sb("wbuf", [128, 18432], BF16)
        NWS = 4
        wst = [sb(f"wst{i}", [128, 512], F32) for i in range(NWS)]
        big = sb("bigbf", [128, 24576], BF16)
        bfa = sb("bfa", [128, 24, 512], BF16)
        NF = 12
        FA = sb("FA", [128, NF * 514], F32)
        Fp = [FA[:, i * 514:(i + 1) * 514] for i in range(NF)]

        def Fpair(i):
            return FA[:, i * 514:(i + 2) * 514].rearrange("p (a c) -> p a c", a=2)
        NCST = 6 * 128
        cbf = sb("cbf", [128, NCST], BF16)
        vec = sb("vecs", [128, DEPTH * NV + 8], F32)
        nbpg = sb("nbpg", [128, DEPTH * 8 + 1], F32)
        epscol = nbpg[:, DEPTH * 8:DEPTH * 8 + 1]
        P2 = [es.enter_context(nc.psum_tensor(f"pp{i}", [128, 1024], F32)) for i in range(4)]
        P = [P2[i // 2][:, (i % 2) * 512:(i % 2 + 1) * 512] for i in range(8)]

        def Ppair(i):
            return P2[i][:, :].rearrange("p (a c) -> p a c", a=2)
        hTt = [bfa[:, 0:8, :], bfa[:, 8:16, :]]
        Bp = [bfa[:, 16 + i, :] for i in range(8)]
        Bx = [bfa[:, i, :] for i in range(8)]

        b_xT = [sc.buf(f"xT{t}") for t in range(4)]
        b_wlo = sc.buf("wbuf_lo")
        b_whi = sc.buf("wbuf_hi")
        b_wst = [sc.buf(f"wst{i}") for i in range(NWS)]
        b_hTt = [sc.buf(f"hTt{i}") for i in range(2)]
        b_F = [sc.buf(f"F{i}") for i in range(NF)]
        b_B = [sc.buf(f"B{i}") for i in range(8)]
        b_Bx = [sc.buf(f"Bx{i}") for i in range(8)]
        b_P = [sc.buf(f"P{i}") for i in range(8)]
        b_cst = sc.buf("cst")
        b_vec = sc.buf("vec")
        b_q = sc.buf("qT")
        b_k = sc.buf("kT")
        b_v = sc.buf("V")
        b_yT = [sc.buf(f"yT{i}") for i in range(2)]
        b_hp = sc.buf("hp")
        b_pTb = sc.buf("pTb")
        b_sq2 = sc.buf("sq2")
        b_hp2 = sc.buf("hp2")
        b_swst = sc.buf("swst")

        qT = big[:, 0:8192].rearrange("p (a t) -> p a t", a=2)
        kT = big[:, 8192:16384].rearrange("p (a t) -> p a t", a=2)
        V = big[:, 16384:24576].rearrange("p (n c) -> p n c", c=256)
        yTt = [big[:, i * 4096:(i + 1) * 4096].rearrange("p (c t) -> p c t", c=8) for i in range(2)]
        hp = big[:, 8192:12288].rearrange("p (c t) -> p c t", c=8)
        pTb = big[:, 16384:17408].rearrange("p (c t) -> p c t", c=2)

        triN = cbf[:, 0:128]
        triC = cbf[:, 128:256]
        ident = cbf[:, 256:384]
        gsum = cbf[:, 384:512]
        onesms = cbf[:, 512:640]
        mask01 = cbf[:, 640:768]

        def vcol(layer, off, n=1):
            base = layer * NV + off
            return vec[:, base:base + n]
        V_NORMG, V_PLEN, V_BPG, V_CW, V_CB, V_BGC, V_BGA = 0, 8, 16, 24, 30, 32, 34
        fgcol = lambda k: vec[:, DEPTH * NV + k: DEPTH * NV + k + 1]

        sc.op(SP, I("dma_start", out=vec[:], in_=vec_d[:, :]), writes=[b_vec], kind="dma", dsem=b_vec.dsem)
        csrc = [(0, 512), (512, 768)]
        for i, (a, b) in enumerate(csrc):
            n = b - a
            sc.op(SP, I("dma_start", out=Fp[i][:, 0:n], in_=cst_d[:, a:b]),
                  writes=[b_F[i]], kind="dma", dsem=b_F[i].dsem)
            sc.op(DVE, I("tensor_copy", out=cbf[:, a:b], in_=Fp[i][:, 0:n]),
                  reads=[b_F[i]], writes=[b_cst], merge_w=True)
        xtok = None
        for t4 in range(4):
            xtok = sc.op(SP, I("dma_start", out=xT[:, :, t4 * 512:(t4 + 1) * 512],
                               in_=xT_d[:, :, t4 * 512:(t4 + 1) * 512]),
                         writes=[b_xT[t4]], kind="dma", dsem=b_xT[t4].dsem, extra=xtok)
        for l_ in range(DEPTH):
            sc.op(DVE, I("tensor_scalar", out=nbpg[:, l_ * 8:(l_ + 1) * 8], in0=vcol(l_, V_BPG, 8), scalar1=-1.0,
                         scalar2=None, op0=ALU.mult), reads=[b_vec], writes=[b_vec], merge_w=True)
        sc.op(DVE, I("memset", epscol, EPS), reads=[b_vec], writes=[b_vec], merge_w=True)

        wst_i = [0]

        def w_piece(src_ap, dst_off, scale_col, eng, wide=False, fslots=None, defer=None):
            fsl = fslots if fslots is not None else (list(range(2, 12)) if wide else [])
            nsl = NWS + len(fsl)
            sl = wst_i[0] % nsl
            wst_i[0] += 1
            if sl < NWS:
                stg, stb = wst[sl][:, :], b_wst[sl]
            else:
                stg, stb = Fp[fsl[sl - NWS]][:, 0:512], b_F[fsl[sl - NWS]]
            sc.op(SP, I("dma_start", out=stg, in_=src_ap), writes=[stb], kind="dma", dsem=stb.dsem)
            dst = wbuf[:, dst_off:dst_off + 512]
            rd = [stb] + ([b_vec] if scale_col is not None else [])
            if eng is ACT:
                if scale_col is None:
                    ins = I("copy", out=dst, in_=stg)
                else:
                    ins = I("mul", out=dst, in_=stg, mul=scale_col)
            else:
                if scale_col is None:
                    ins = I("tensor_copy", out=dst, in_=stg)
                else:
                    ins = I("tensor_scalar", out=dst, in0=stg, scalar1=scale_col, scalar2=None,
                            op0=ALU.mult)
            def cast():
                sc.op(eng, ins, reads=rd, writes=[b_wlo if dst_off < 8192 else b_whi], merge_w=True, no_waw=True)
            if defer is None:
                cast()
            else:
                defer.append(cast)

        def win_pieces(layer):
            out = []
            for k in range(8):
                for c0 in range(0, 2048, 512):
                    out.append((win_d[layer, :, k, c0:c0 + 512], k * 2048 + c0, vcol(layer, V_NORMG + k)))
            return out

        def wd_pieces(layer):
            out = []
            for k in range(8):
                for c0 in range(0, 1024, 512):
                    out.append((wout_d[layer, :, k, c0:c0 + 512], k * 1024 + c0, None))
            for k in range(8):
                for c0 in range(0, 1024, 512):
                    out.append((wpg_d[layer, :, k, c0:c0 + 512], 8192 + k * 1024 + c0, vcol(layer, V_PLEN + k)))
            for j in range(2):
                for c0 in range(0, 1024, 512):
                    out.append((wpe_d[layer, :, j, c0:c0 + 512], 16384 + j * 1024 + c0, None))
            return out

        def load_win(layer, lo=0, hi=32, engs=None):
            engs = engs or [ACT, DVE]
            for i, (src, off, scol) in enumerate(win_pieces(layer)):
                if lo <= i < hi:
                    w_piece(src, off, scol, engs[i % len(engs)], wide=True)

        win = wbuf[:, 0:16384].rearrange("p (k c) -> p k c", k=8)
        wout = wbuf[:, 0:8192].rearrange("p (k c) -> p k c", k=8)
        wpg = wbuf[:, 8192:16384].rearrange("p (k c) -> p k c", k=8)
        wpe = wbuf[:, 16384:18432].rearrange("p (k c) -> p k c", k=2)

        half_cache = {}

        def rstd_from(psb, pst, fb, f):
            sc.op(ACT, I("activation", out=f[:, 0:512], in_=pst[:, :], func=AF.Ln, bias=epscol),
                  reads=[psb, b_vec], writes=[fb])
            sc.op(ACT, I("activation", out=f[:, 0:512], in_=f[:, 0:512], func=AF.Exp, scale=-0.5),
                  reads=[fb], writes=[fb])

        def sigmoid_act(pst, psb, f, fb, nbias=None):
            kw = {} if nbias is None else {"bias": nbias}
            rd = [psb] + ([] if nbias is None else [b_vec])
            sc.op(ACT, I("activation", out=f[:, 0:512], in_=pst[:, :], func=AF.Exp, scale=-1.0, **kw),
                  reads=rd, writes=[fb])
            sc.op(ACT, I("activation", out=f[:, 0:512], in_=f[:, 0:512], func=AF.Ln, bias=1.0),
                  reads=[fb], writes=[fb])
            sc.op(ACT, I("activation", out=f[:, 0:512], in_=f[:, 0:512], func=AF.Exp, scale=-1.0),
                  reads=[fb], writes=[fb])

        def norm_sq(t4, sq, sq_b):
            t0 = t4 * 512
            merge(b_sq2.r, sq_b.r)
            merge(b_sq2.w, sq_b.w)
            for hf in range(2):
                hb_ = sq_b if hf == 0 else b_sq2
                sc.op(ACT, I("activation", out=sq[:, hf * 4:(hf + 1) * 4, :], in_=xT[:, hf * 4:(hf + 1) * 4, t0:t0 + 512],
                             func=AF.Square), reads=[b_xT[t4]], writes=[hb_])

        def norm_ms(sq, sq_b, psi):
            for k in range(8):
                hb_ = sq_b if k < 4 else b_sq2
                tok = sc.op(PE, I("matmul", P[psi][:, :], lhsT=onesms, rhs=sq[:, k, :], start=(k == 0),
                                  stop=(k == 7)), reads=[hb_, b_cst], writes=[b_P[psi]])
                if k >= 4:
                    merge(sq_b.r, tok)
            merge(sq_b.w, b_sq2.w)

        def norm_front(t4, sq, sq_b, psi, fi):
            norm_sq(t4, sq, sq_b)
            norm_ms(sq, sq_b, psi)
            rstd_from(b_P[psi], P[psi], b_F[fi], Fp[fi])

        def norm_back(t4, dst, dst_b, fi):
            t0 = t4 * 512
            for k in range(8):
                sc.op(DVE, I("tensor_tensor", out=dst[:, k, :], in0=xT[:, k, t0:t0 + 512],
                             in1=Fp[fi][:, 0:512], op=ALU.mult),
                      reads=[b_xT[t4], b_F[fi]], writes=[dst_b], merge_w=(k > 0))

        def norm_tile(t4, dst, dst_b, sq, sq_b, psi, fi):
            norm_front(t4, sq, sq_b, psi, fi)
            norm_back(t4, dst, dst_b, fi)

        def gather(src_d, src_b, dst_d, dst_b):
            sc.op(POOL, I("collective_compute",
                          "AllGather", ALU.bypass, replica_groups=[[0, 1], [2, 3], [4, 5], [6, 7]],
                          ins=[src_d.ap().opt()], outs=[dst_d.ap().opt()]),
                  reads=[src_b], writes=[dst_b], kind="cc", dsem=dst_b.dsem)

        def phase_A_back(t4, stq=None):
            norm_back(t4, hTt[1], b_hTt[1], 11)
            sc.op(stq or SP, I("dma_start", out=hb_d[t4][:, :].rearrange("(k p) t -> p k t", p=128), in_=hTt[1][:, :, :]),
                  reads=[b_hTt[1]], writes=[hb_b[t4]], kind="dma",
                  dsem=(b_swst.dsem if stq is POOL else b_hTt[1].dsem))
            gather(hb_d[t4], hb_b[t4], hg_d[t4], hg_b[t4])

        def phase_A_tile(t4, stq=None):
            norm_front(t4, hTt[0], b_hTt[0], 7, 11)
            phase_A_back(t4, stq)

        def load_y(t4, extra=None):
            ys = t4 % 2
            yT = yTt[ys]
            t0 = t4 * 512
            plan = [(0, 0, 0, 2), (0, 1, 2, 2), (1, 0, 4, 1), (1, 1, 6, 1), (2 + t4, 0, 5, 1), (2 + t4, 1, 7, 1)]
            for pi, (part, r, c0, nch) in enumerate(plan):
                def ld(e, part=part, r=r, c0=c0, nch=nch, yT=yT, t0=t0):
                    if "half_sp" not in half_cache:
                        half_cache["half_sp"] = e.snap(e.partition_id() % 2, min_val=0, max_val=1)
                    half = half_cache["half_sp"]
                    if part < 2:
                        ygv = yg_d[part].ap().rearrange("(r c p) (hh t) -> r hh p c t", r=2, c=nch, p=128, hh=2)
                        src = ygv[r, bass.ds(half, 1), :, :, t0:t0 + 512]
                    else:
                        ygv = yg_d[part].ap().rearrange("(r c p) (hh t) -> r hh p c t", r=2, c=1, p=128, hh=2)
                        src = ygv[r, bass.ds(half, 1), :, :, :]
                    return e.dma_start(out=yT[:, c0:c0 + nch, :], in_=src)
                sc.op(SP, ld, reads=[yg_b[part]], writes=[b_yT[ys]], kind="dma", dsem=b_yT[ys].dsem,
                      merge_w=(pi > 0), extra=extra)


        def phase_B(layer):
            for cc in range(2):
                sc.op(POOL, I("memset", Fp[6 + cc][:, 0:2], 0.0), writes=[b_F[6 + cc]])
            rot = [0]

            def load_hT(tt):
                hs_, half_ = tt % 2, tt // 4
                sc.op(SP, I("dma_start", out=hTt[hs_][:, :, :],
                            in_=hg_d[tt % 4][half_ * D:(half_ + 1) * D, :].rearrange("(k p) t -> p k t", p=128)),
                      reads=[hg_b[tt % 4]], writes=[b_hTt[hs_]], kind="dma", dsem=b_hTt[hs_].dsem)

            for tt in range(8):
                hs = tt % 2
                hT = hTt[hs]
                half, tl = tt // 4, (tt % 4) * 512
                tok0 = tt * 512
                if tt == 0:
                    load_hT(0)

                def proj(psi, col0):
                    for k in range(8):
                        sc.op(PE, I("matmul", P[psi][:, :], lhsT=win[:, k, col0:col0 + 128],
                                    rhs=hT[:, k, :], start=(k == 0), stop=(k == 7)),
                              reads=[b_hTt[hs], b_wlo, b_whi], writes=[b_P[psi]])

                def nextrot():
                    psi = 5 + rot[0] % 3
                    rot[0] += 1
                    return psi

                def conv_front(cc):
                    for si in range(4):
                        proj(si, si * 256 + cc * 128)
                    u, ub = Fp[6 + cc], b_F[6 + cc]
                    w_ = lambda j: vcol(layer, V_CW + cc * 3 + j)
                    sc.op(ACT, I("activation", out=Fp[0][:, 0:512], in_=P[1][:, :], func=AF.Copy),
                          reads=[b_P[1]], writes=[b_F[0]])
                    sigmoid_act(P[3], b_P[3], Fp[5], b_F[5])
                    sc.op(DVE, I("tensor_tensor", out=u[:, 2:514], in0=Fp[0][:, 0:512], in1=P[2][:, :], op=ALU.mult),
                          reads=[b_F[0], b_P[2]], writes=[ub], merge_w=True)
                    sc.op(DVE, I("tensor_scalar", out=Fp[1][:, 0:512], in0=u[:, 2:514], scalar1=w_(2),
                                 scalar2=vcol(layer, V_CB + cc), op0=ALU.mult, op1=ALU.add),
                          reads=[ub, b_vec], writes=[b_F[1]])
                    sc.op(DVE, I("scalar_tensor_tensor", out=Fp[2][:, 0:512], in0=u[:, 1:513], scalar=w_(1),
                                 in1=Fp[1][:, 0:512], op0=ALU.mult, op1=ALU.add),
                          reads=[ub, b_vec, b_F[1]], writes=[b_F[2]])
                    sc.op(DVE, I("scalar_tensor_tensor", out=Fp[1][:, 0:512], in0=u[:, 0:512], scalar=w_(0),
                                 in1=Fp[2][:, 0:512], op0=ALU.mult, op1=ALU.add),
                          reads=[ub, b_vec, b_F[2]], writes=[b_F[1]])
                    sc.op(POOL, I("tensor_copy", out=u[:, 0:2], in_=u[:, 512:514]), reads=[ub], writes=[ub])
                    sc.op(DVE, I("tensor_tensor", out=Fp[3][:, 0:512], in0=Fp[1][:, 0:512], in1=P[0][:, :],
                                 op=ALU.mult), reads=[b_F[1], b_P[0]], writes=[b_F[3]])
                    sc.op(POOL, I("tensor_tensor", out=Bp[0][:, :], in0=Fp[3][:, 0:512], in1=Fp[3][:, 0:512],
                                  op=ALU.mult), reads=[b_F[3]], writes=[b_B[0]])
                    sc.op(DVE, I("tensor_tensor", out=Fp[2][:, 0:512], in0=P[3][:, :], in1=Fp[5][:, 0:512],
                                 op=ALU.mult), reads=[b_P[3], b_F[5]], writes=[b_F[2]])

                def conv_back(cc):
                    sc.op(PE, I("matmul", P[4][:, :], lhsT=gsum, rhs=Bp[0][:, :], start=True, stop=True),
                          reads=[b_B[0], b_cst], writes=[b_P[4]])
                    rstd_from(b_P[4], P[4], b_F[4], Fp[4])
                    sc.op(DVE, I("tensor_tensor", out=Fp[3][:, 0:512], in0=Fp[3][:, 0:512], in1=Fp[4][:, 0:512],
                                 op=ALU.mult), reads=[b_F[3], b_F[4]], writes=[b_F[3]])
                    ob = 1 + cc
                    sc.op(DVE, I("scalar_tensor_tensor", out=Bp[ob][:, :], in0=Fp[3][:, 0:512],
                                 scalar=vcol(layer, V_BGC + cc), in1=Fp[2][:, 0:512], op0=ALU.mult, op1=ALU.mult),
                          reads=[b_F[3], b_F[2], b_vec], writes=[b_B[ob]])
                    sc.op(SP, I("dma_start", out=yb_d[0][cc * 128:(cc + 1) * 128, tok0:tok0 + 512], in_=Bp[ob][:, :]),
                          reads=[b_B[ob]], writes=[yb_b[0]], kind="dma", dsem=b_B[ob].dsem, merge_w=True)

                def qk_part():
                    for pair in range(2):
                        psi = nextrot()
                        proj(psi, 1024 + pair * 128)
                        sc.op(ACT, I("activation", out=qT[:, pair, tok0:tok0 + 512], in_=P[psi][:, :], func=AF.Copy,
                                     scale=0.125), reads=[b_P[psi]], writes=[b_q], merge_w=True)
                        psi = nextrot()
                        proj(psi, 1280 + pair * 128)
                        sc.op(ACT, I("activation", out=kT[:, pair, tok0:tok0 + 512], in_=P[psi][:, :], func=AF.Copy),
                              reads=[b_P[psi]], writes=[b_k], merge_w=True)

                def azv_part():
                    for bp in range(2):
                        psi = nextrot()
                        for bl in range(2):
                            blk = bp * 2 + bl
                            for k in range(8):
                                sc.op(PE, I("matmul", P[psi][:, bl * 256:(bl + 1) * 256],
                                            lhsT=hT[:, k, blk * 128:(blk + 1) * 128],
                                            rhs=win[:, k, 1536:1792], start=(k == 0), stop=(k == 7)),
                                      reads=[b_hTt[hs], b_wlo, b_whi], writes=[b_P[psi]])
                        sc.op(ACT, I("activation", out=V[:, tt * 4 + bp * 2: tt * 4 + bp * 2 + 2, :],
                                     in_=P[psi][:, :].rearrange("p (n c) -> p n c", c=256), func=AF.Copy),
                              reads=[b_P[psi]], writes=[b_v], merge_w=True)
                    for pair in range(2):
                        psi = nextrot()
                        proj(psi, 1792 + pair * 128)
                        gb = 3 + pair
                        fe, fe_b = Fp[8 + pair], b_F[8 + pair]
                        sigmoid_act(P[psi], b_P[psi], fe, fe_b)
                        sc.op(DVE, I("tensor_tensor", out=Bp[gb][:, :], in0=P[psi][:, :], in1=fe[:, 0:512],
                                     op=ALU.mult), reads=[b_P[psi], fe_b], writes=[b_B[gb]])
                        sc.op(SP, I("dma_start", out=gz_d[pair * 128:(pair + 1) * 128, tok0:tok0 + 512],
                                    in_=Bp[gb][:, :]),
                              reads=[b_B[gb]], writes=[gz_b], kind="dma", dsem=b_B[gb].dsem, merge_w=True)

                conv_front(0)
                if tt + 1 < 8:
                    load_hT(tt + 1)
                qk_part()
                conv_back(0)
                conv_front(1)
                azv_part()
                conv_back(1)

        def phase_C(layer, wd_list):
            for i in range(8):
                merge(b_Bx[i].r, b_hTt[0].r)
                merge(b_Bx[i].r, b_hTt[0].w)
                merge(b_Bx[i].w, b_hTt[0].w)
            streams = []
            for pair in range(2):
                for qi in range(8):
                    nkb = 4 * (qi + 1)
                    streams.append([(pair, qi, kb, i == 0, kb == 0) for i, kb in enumerate(range(nkb - 1, -1, -1))])
            sblocks = []
            pre = [False] * len(streams)
            for i, st in enumerate(streams):
                pend = st[4:] if pre[i] else st
                nfull = len(st) - 4
                if i + 1 < len(streams) and nfull >= 4 and len(streams[i + 1]) > 4:
                    sblocks += pend[:-4]
                    for j in range(4):
                        sblocks.append(pend[len(pend) - 4 + j])
                        sblocks.append(streams[i + 1][j])
                    pre[i + 1] = True
                else:
                    sblocks += pend
            ns = len(sblocks)
            assert ns == sum(len(st) for st in streams)

            def cbp(s):
                return 2 if (sblocks[s][0] * 8 + sblocks[s][1]) % 2 == 0 else 1
            OB = [6, 7]
            eF = [[0, 2, 4]]
            wF = [[6, 8]]
            F_O, F_R = 10, 11
            Lb = [[(Bp[0], b_B[0]), (Bp[1], b_B[1])], [(Bp[2], b_B[2]), (Bp[3], b_B[3])]]
            Ab = [[(Bp[4], b_B[4]), (Bp[5], b_B[5])], [(Bp[6], b_B[6]), (Bp[7], b_B[7])]]
            SQ = (Bx[0], b_Bx[0])
            GT = [(Bx[1], b_Bx[1]), (Bx[2], b_Bx[2])]
            YA = [(Bx[3], b_Bx[3]), (Bx[4], b_Bx[4])]

            def c0of(s):
                pair, qi, kb, first, last = sblocks[s]
                r = kb - 4 * qi
                return r * 128 if r > 0 else 0

            def qk(s, hh):
                pair, qi, kb, first, last = sblocks[s]
                z = hh
                c0 = c0of(s)
                pr = slice(hh * 64, (hh + 1) * 64)
                sc.op(PE, I("matmul", P[z][:, c0:512], lhsT=kT[pr, pair, kb * 128:(kb + 1) * 128],
                            rhs=qT[pr, pair, qi * 512 + c0:(qi + 1) * 512], start=True, stop=True),
                      reads=[b_q, b_k], writes=[b_P[z]])

            def exp_z(s):
                c0 = c0of(s)
                f = eF[0][s % 3]
                sc.op(ACT, I("activation", out=Fpair(f)[:, :, c0:512], in_=Ppair(0)[:, :, c0:512], func=AF.Exp),
                      reads=[b_P[0], b_P[1]], writes=[b_F[f], b_F[f + 1]])
                pair_, qi_, kb_ = sblocks[s][0], sblocks[s][1], sblocks[s][2]
                if kb_ >= 4 * qi_:
                    for hh in range(2):
                        sc.op(DVE, I("tensor_tensor", out=Fp[f + hh][:, c0:c0 + 128], in0=Fp[f + hh][:, c0:c0 + 128],
                                     in1=mask01, op=ALU.mult),
                              reads=[b_F[f + hh], b_cst], writes=[b_F[f + hh]])

            def ln_l(s):
                c0 = c0of(s)
                f = eF[0][s % 3]
                li = 2 * (s % 2)
                sc.op(ACT, I("activation", out=bfa[:, 16 + li:16 + li + 2, c0:512], in_=Fpair(f)[:, :, c0:512],
                             func=AF.Ln, bias=1.0),
                      reads=[b_F[f], b_F[f + 1]], writes=[b_B[li], b_B[li + 1]])

            def tri(s, hh):
                first = sblocks[s][3]
                c0 = c0of(s)
                li = 2 * (s % 2) + hh
                cb = 2 * cbp(s) + hh
                sc.op(PE, I("matmul", P[cb][:, c0:512], lhsT=triN, rhs=Bp[li][:, c0:512], start=first, stop=True,
                            skip_group_check=True),
                      reads=[b_B[li], b_cst], writes=[b_P[cb]])

            def tric(s, hh):
                c0 = c0of(s)
                li = 2 * (s % 2) + hh
                cb = 2 * cbp(s) + hh
                sc.op(PE, I("matmul", P[cb][:, c0:512], lhsT=triC, rhs=Bp[li][:, c0:512], start=False, stop=True,
                            skip_group_check=True),
                      reads=[b_B[li], b_cst], writes=[b_P[cb]])

            def w_exp(s):
                c0 = c0of(s)
                f = wF[0][s % 2]
                cp_ = cbp(s)
                sc.op(ACT, I("activation", out=Fpair(f)[:, :, c0:512], in_=Ppair(cp_)[:, :, c0:512], func=AF.Exp),
                      reads=[b_P[2 * cp_], b_P[2 * cp_ + 1]], writes=[b_F[f], b_F[f + 1]])

            def mul_a(s):
                c0 = c0of(s)
                fe, fw = eF[0][s % 3], wF[0][s % 2]
                ai = 4 + 2 * (s % 2)
                sc.op(DVE, I("tensor_tensor", out=bfa[:, 16 + ai:16 + ai + 2, c0:512], in0=Fpair(fe)[:, :, c0:512],
                             in1=Fpair(fw)[:, :, c0:512], op=ALU.mult),
                      reads=[b_F[fe], b_F[fe + 1], b_F[fw], b_F[fw + 1]], writes=[b_B[ai], b_B[ai + 1]])

            def av(s, hh):
                pair, qi, kb, first, last = sblocks[s]
                c0 = c0of(s)
                ai = 4 + 2 * (s % 2) + hh
                o = OB[(pair * 8 + qi) % 2]
                h4 = pair * 2 + hh
                sc.op(PE, I("matmul", P[o][hh * 64:(hh + 1) * 64, c0:512], lhsT=V[:, kb, h4 * 64:(h4 + 1) * 64],
                            rhs=Bp[ai][:, c0:512], start=first, stop=last, skip_group_check=True),
                      reads=[b_v, b_B[ai]], writes=[b_P[o]])

            def epilogue1(pair, qi):
                o = OB[(pair * 8 + qi) % 2]
                tok0 = qi * 512
                gt, gb = GT[(pair * 8 + qi) % 2]
                sc.op(SP, I("dma_start", out=gt[:, :], in_=gz_d[pair * 128:(pair + 1) * 128, tok0:tok0 + 512]),
                      reads=[gz_b], writes=[gb], kind="dma", dsem=gb.dsem)
                sc.op(DVE, I("tensor_copy", out=Fp[F_O][:, 0:512], in_=P[o][:, :]),
                      reads=[b_P[o]], writes=[b_F[F_O]])
                sc.op(DVE, I("tensor_tensor", out=SQ[0][:, :], in0=Fp[F_O][:, 0:512], in1=Fp[F_O][:, 0:512],
                             op=ALU.mult), reads=[b_F[F_O]], writes=[SQ[1]])

            def epilogue1b(pair, qi):
                o = OB[(pair * 8 + qi) % 2]
                sc.op(PE, I("matmul", P[o][:, :], lhsT=gsum, rhs=SQ[0][:, :], start=True, stop=True),
                      reads=[SQ[1], b_cst], writes=[b_P[o]])

            def epilogue2(pair, qi):
                o = OB[(pair * 8 + qi) % 2]
                tok0 = qi * 512
                gt, gb = GT[(pair * 8 + qi) % 2]
                yt, yb_ = YA[(pair * 8 + qi) % 2]
                rstd_from(b_P[o], P[o], b_F[F_R], Fp[F_R])
                sc.op(DVE, I("tensor_tensor", out=Fp[F_O][:, 0:512], in0=Fp[F_O][:, 0:512], in1=Fp[F_R][:, 0:512],
                             op=ALU.mult), reads=[b_F[F_O], b_F[F_R]], writes=[b_F[F_O]])
                sc.op(DVE, I("scalar_tensor_tensor", out=yt[:, :], in0=Fp[F_O][:, 0:512],
                             scalar=vcol(layer, V_BGA + pair), in1=gt[:, :], op0=ALU.mult, op1=ALU.mult),
                      reads=[b_F[F_O], gb, b_vec], writes=[yb_])
                if pair == 0:
                    part, dst = 1, yb_d[1][:, tok0:tok0 + 512]
                else:
                    part = 2 + qi % 4
                    dst = yb_d[part][:, (qi // 4) * 512:(qi // 4 + 1) * 512]
                sc.op(SP, I("dma_start", out=dst, in_=yt[:, :]),
                      reads=[yb_], writes=[yb_b[part]], kind="dma", dsem=yb_.dsem, merge_w=True)
                if pair == 0 and qi == 7:
                    gather(yb_d[1], yb_b[1], yg_d[1], yg_b[1])
                if pair == 1 and qi >= 4:
                    gather(yb_d[part], yb_b[part], yg_d[part], yg_b[part])
                if pair == 1 and qi == 5:
                    load_y(0, extra=dict(b_q.r))

            wd_it = iter(wd_list)
            pending = []
            for hh in range(2):
                qk(0, hh)
            for s in range(ns + 1):
                if s < ns:
                    exp_z(s)
                if s + 1 < ns:
                    for hh in range(2):
                        qk(s + 1, hh)
                if s >= 1:
                    w_exp(s - 1)
                if s < ns:
                    ln_l(s)
                if s >= 1 and not sblocks[s - 1][4]:
                    for hh in range(2):
                        tric(s - 1, hh)
                if s < ns:
                    for hh in range(2):
                        tri(s, hh)
                if s >= 1:
                    mul_a(s - 1)
                    for hh in range(2):
                        av(s - 1, hh)
                    if sblocks[s - 1][4]:
                        epilogue1(sblocks[s - 1][0], sblocks[s - 1][1])
                        pending.append((s + 1, 0, sblocks[s - 1][0], sblocks[s - 1][1]))
                        pending.append((s + 3, 1, sblocks[s - 1][0], sblocks[s - 1][1]))
                        pending.sort()
                while pending and pending[0][0] <= s:
                    _, st_, p_, q_ = pending.pop(0)
                    (epilogue1b if st_ == 0 else epilogue2)(p_, q_)
                if s % 6 == 3:
                    nxt = next(wd_it, None)
                    if nxt is not None:
                        w_piece(nxt[0], nxt[1], nxt[2], DVE)
            for _, st_, p_, q_ in sorted(pending):
                (epilogue1b if st_ == 0 else epilogue2)(p_, q_)
            for nxt in wd_it:
                w_piece(nxt[0], nxt[1], nxt[2], DVE)

        def phase_D(layer):
            last_layer = layer == DEPTH - 1
            pTbs = [big[:, 16384 + i * 1024: 16384 + (i + 1) * 1024].rearrange("p (c t) -> p c t", c=2)
                    for i in range(2)]
            b_pT2 = [b_pTb, b_v]

            def load_p(t4):
                t0 = t4 * 512
                for j in range(2):
                    sc.op(SP, I("dma_start", out=Fp[j][:, 0:512], in_=pT_d[layer, :, j, t0:t0 + 512]),
                          writes=[b_F[j]], kind="dma", dsem=b_F[j].dsem)
                    sc.op(ACT, I("activation", out=pTbs[t4 % 2][:, j, :], in_=Fp[j][:, 0:512], func=AF.Copy),
                          reads=[b_F[j]], writes=[b_pT2[t4 % 2]], merge_w=(j > 0))

            def outproj(t4, n0, n1):
                t0 = t4 * 512
                ys = t4 % 2
                yT = yTt[ys]
                for n in range(n0, n1):
                    psi = n % 2
                    for c in range(8):
                        sc.op(PE, I("matmul", P[psi][:, :], lhsT=wout[:, c, n * 128:(n + 1) * 128], rhs=yT[:, c, :],
                                    start=(c == 0), stop=(c == 7)), reads=[b_yT[ys], b_wlo], writes=[b_P[psi]])
                    sc.op(DVE, I("tensor_tensor", out=xT[:, n, t0:t0 + 512], in0=xT[:, n, t0:t0 + 512],
                                 in1=P[psi][:, :], op=ALU.add),
                          reads=[b_P[psi], b_xT[t4]], writes=[b_xT[t4]])

            hps = [hp, big[:, 12288:16384].rearrange("p (c t) -> p c t", c=8)]
            b_hps = [b_hp, b_hp2]

            def gate(t4, n0=0, n1=8):
                t0 = t4 * 512
                pTb_ = pTbs[t4 % 2]
                hp_, b_hp_ = hps[t4 % 2], b_hps[t4 % 2]
                for n in range(n0, n1):
                    pg, pe_ = 2 + (n % 2) * 2, 3 + (n % 2) * 2
                    for k in range(8):
                        sc.op(PE, I("matmul", P[pg][:, :], lhsT=wpg[:, k, n * 128:(n + 1) * 128], rhs=hp_[:, k, :],
                                    start=(k == 0), stop=(k == 7)), reads=[b_hp_, b_whi], writes=[b_P[pg]])
                    for j in range(2):
                        sc.op(PE, I("matmul", P[pe_][:, :], lhsT=wpe[:, j, n * 128:(n + 1) * 128], rhs=pTb_[:, j, :],
                                    start=(j == 0), stop=(j == 1)), reads=[b_pT2[t4 % 2], b_whi], writes=[b_P[pe_]])
                    f = 2 + n % 2
                    sigmoid_act(P[pg], b_P[pg], Fp[f], b_F[f], nbias=nbpg[:, layer * 8 + n: layer * 8 + n + 1])
                    sc.op(DVE, I("tensor_tensor", out=Fp[f][:, 0:512], in0=Fp[f][:, 0:512], in1=P[pe_][:, :],
                                 op=ALU.mult), reads=[b_F[f], b_P[pe_]], writes=[b_F[f]])
                    sc.op(DVE, I("tensor_tensor", out=xT[:, n, t0:t0 + 512], in0=xT[:, n, t0:t0 + 512],
                                 in1=Fp[f][:, 0:512], op=ALU.add),
                          reads=[b_F[f], b_xT[t4]], writes=[b_xT[t4]])

            def final_back(t4):
                t0 = t4 * 512
                for k in range(8):
                    sc.op(DVE, I("scalar_tensor_tensor", out=xT[:, k, t0:t0 + 512], in0=xT[:, k, t0:t0 + 512],
                                 scalar=fgcol(k), in1=Fp[11][:, 0:512], op0=ALU.mult, op1=ALU.mult),
                          reads=[b_xT[t4], b_F[11], b_vec], writes=[b_xT[t4]])
                for hf in range(2):
                    sc.op(SP, I("dma_start", out=outT_d[:, hf * 4:(hf + 1) * 4, t0:t0 + 512],
                                in_=xT[:, hf * 4:(hf + 1) * 4, t0:t0 + 512]),
                          reads=[b_xT[t4]], writes=[out_b], kind="dma", dsem=out_b.dsem, merge_w=True)

            ew = {"on": False, "i": 0}
            late_casts = []
            nxt_pieces = win_pieces(layer + 1) if not last_layer else []

            def early_w(n, fslots=None, limit=16, defer=None):
                if not ew["on"]:
                    return
                for _ in range(n):
                    if ew["i"] < min(limit, len(nxt_pieces)):
                        src, off, scol = nxt_pieces[ew["i"]]
                        w_piece(src, off, scol, ACT if ew["i"] % 2 == 0 else DVE, fslots=fslots, defer=defer)
                        ew["i"] += 1

            load_p(0)
            load_y(1)
            outproj(0, 0, 8)
            load_p(1)
            load_y(2)
            norm_front(0, hTt[0], b_hTt[0], 7, 7)
            norm_back(0, hps[0], b_hps[0], 7)
            outproj(1, 0, 8)
            load_y(3)
            for t4 in range(4):
                nt = t4 + 1 < 4
                if nt:
                    norm_sq(t4 + 1, hTt[0], b_hTt[0])
                gate(t4, 0, 2)
                if nt:
                    norm_ms(hTt[0], b_hTt[0], 7)
                    rstd_from(b_P[7], P[7], b_F[7], Fp[7])
                gate(t4, 2, 4)
                early_w(4)
                if nt:
                    norm_back(t4 + 1, hps[(t4 + 1) % 2], b_hps[(t4 + 1) % 2], 7)
                gate(t4, 4, 8)
                early_w(4)
                if t4 == 3:
                    early_w(15, fslots=[4, 5, 6, 8, 9, 10, 7, 0, 1, 2, 3], limit=32, defer=late_casts)
                norm_sq(t4, hTt[0], b_hTt[0])
                if t4 + 2 < 4:
                    load_p(t4 + 2)
                    outproj(t4 + 2, 0, 2)
                norm_ms(hTt[0], b_hTt[0], 6)
                rstd_from(b_P[6], P[6], b_F[11], Fp[11])
                if t4 + 2 < 4:
                    outproj(t4 + 2, 2, 8)
                    if t4 + 2 == 3:
                        ew["on"] = True
                if not last_layer:
                    phase_A_back(t4)
                else:
                    final_back(t4)
            for c_ in late_casts:
                c_()
            return ew["i"]

        early_done = [0]

        class _Stop(Exception):
            pass
        nst = [0]

        def stage():
            nst[0] += 1
            if stop_after is not None and nst[0] > stop_after:
                raise _Stop()
        try:
            for t4 in range(4):
                phase_A_tile(t4, stq=POOL)
                load_win(0, t4 * 8, (t4 + 1) * 8)
            for layer in range(DEPTH):
                stage()
                xch = [b_.dsem for b_ in yg_b + hg_b]
                if layer > 0:
                    sc.barrier(exclude=xch)
                    load_win(layer, early_done[0], 32)
                stage()
                phase_B(layer)
                gather(yb_d[0], yb_b[0], yg_d[0], yg_b[0])
                stage()
                phase_C(layer, wd_pieces(layer))
                stage()
                sc.barrier(exclude=xch)
                stage()
                early_done[0] = phase_D(layer)
        except _Stop:
            pass
        if debug:
            sc.barrier()
            d1 = sc.buf("dbg1")
            for i in range(4):
                sc.op(SP, I("dma_start", out=dbg["hg"][i][:, :], in_=hg_d[i][:, :]), writes=[d1], kind="dma",
                      dsem=d1.dsem, merge_w=True)
            for i in range(NY):
                sc.op(SP, I("dma_start", out=dbg["yg"][i][:, :], in_=yg_d[i][:, :]), writes=[d1], kind="dma",
                      dsem=d1.dsem, merge_w=True)
        sc.barrier()

        for s_ in sc.sems():
            s_.h = es.enter_context(nc.semaphore(s_.name))
        with nc.Block() as block:
            @block.tensor
            def _(eng):
                replay(PE, eng)

            @block.scalar
            def _(eng):
                replay(ACT, eng)

            @block.vector
            def _(eng):
                replay(DVE, eng)

            @block.gpsimd
            def _(eng):
                replay(POOL, eng)

            @block.sync
            def _(eng):
                replay(SP, eng)
    return nc


def _chunk_rows(a):
    k = a.shape[0] // 128
    return np.ascontiguousarray(a.reshape(k, 128, a.shape[1]).transpose(1, 0, 2))


def _consts():
    j = np.arange(128)[:, None]
    s = np.arange(128)[None, :]
    triN = np.where(j >= s, -1.0, 0.0)
    triC = np.where(j < s, -1.0, 0.0)
    ident = np.eye(128)
    gsum = np.where((j // 64) == (s // 64), 1.0 / 64, 0.0)
    onesms = np.full((128, 128), 1.0 / 1024)
    mask01 = np.where(j < s, 1.0, 0.0)
    return np.ascontiguousarray(np.concatenate([triN, triC, ident, gsum, onesms, mask01], axis=1).astype(np.float32))


def prep_inputs(x, p, norm_g, w_in, conv_w, conv_b, branch_g, w_out, ple_norm_g, w_pg, b_pg, w_pe, final_g):
    f = lambda a: np.asarray(a, dtype=np.float32)
    x, p, norm_g, w_in, conv_w, conv_b, branch_g, w_out, ple_norm_g, w_pg, b_pg, w_pe, final_g = map(
        f, (x, p, norm_g, w_in, conv_w, conv_b, branch_g, w_out, ple_norm_g, w_pg, b_pg, w_pe, final_g))
    cst = _consts()
    per_g = []
    for g in range(2):
        cols = np.concatenate([np.arange(seg * 512 + g * 256, seg * 512 + (g + 1) * 256) for seg in range(8)])
        win = np.stack([_chunk_rows(w_in[i][:, cols]) for i in range(DEPTH)])
        vecs = []
        for i in range(DEPTH):
            cw = conv_w[i][:, g * 256:(g + 1) * 256].reshape(3, 2, 128).transpose(2, 1, 0).reshape(128, 6)
            v = np.concatenate([
                norm_g[i].reshape(8, 128).T, ple_norm_g[i].reshape(8, 128).T, b_pg[i].reshape(8, 128).T,
                cw, conv_b[i][g * 256:(g + 1) * 256].reshape(2, 128).T,
                branch_g[i][g * 256:(g + 1) * 256].reshape(2, 128).T,
                branch_g[i][512 + g * 256: 512 + (g + 1) * 256].reshape(2, 128).T], axis=1)
            vecs.append(v)
        vecs.append(final_g.reshape(8, 128).T)
        per_g.append((win, np.ascontiguousarray(np.concatenate(vecs, axis=1).astype(np.float32))))
    wout = np.stack([_chunk_rows(w_out[i]) for i in range(DEPTH)])
    wpg = np.stack([_chunk_rows(w_pg[i]) for i in range(DEPTH)])
    wpe = np.stack([_chunk_rows(w_pe[i]) for i in range(DEPTH)])
    maps = []
    for c in range(NCORES):
        b, g = c // 2, c % 2
        xT = _chunk_rows(np.ascontiguousarray(x[b, g * HALF:(g + 1) * HALF, :].T))
        pT = np.stack([_chunk_rows(np.ascontiguousarray(p[i, b, g * HALF:(g + 1) * HALF, :].T)) for i in range(DEPTH)])
        maps.append({"xT": xT, "pT": pT, "win": per_g[g][0], "wout": wout, "wpg": wpg, "wpe": wpe,
                     "vec": per_g[g][1], "cst": cst})
    return maps


def assemble(results):
    out = np.empty((4, S, D), np.float32)
    for c in range(NCORES):
        b, g = c // 2, c % 2
        oT = np.asarray(results[c]["outT"])
        out[b, g * HALF:(g + 1) * HALF, :] = oT.transpose(2, 1, 0).reshape(HALF, D)
    return out


_NC_CACHE = {}


def kernel(**inputs):
    maps = prep_inputs(**inputs)
    if "nc" not in _NC_CACHE:
        _NC_CACHE["nc"] = build_program()
    res = run_bass_kernel_spmd(_NC_CACHE["nc"], maps, core_ids=list(range(NCORES)))
    return assemble(res.results)
```

```python
import contextlib
import numpy as np
import concourse.bass as bass
import concourse.mybir as mybir
from concourse.bass_utils import run_bass_kernel_spmd

F32 = mybir.dt.float32
BF16 = mybir.dt.bfloat16
AF = mybir.ActivationFunctionType
ALU = mybir.AluOpType

DEPTH = 2
S = 4096
HALF = 2048
D = 1024
NEG = -30000.0
EPS = 1e-6
NCORES = 8


class Sem:
    def __init__(self, name):
        self.name = name
        self.h = None
        self.val = 0


class Q:
    def __init__(self, name):
        self.name = name
        self.ops = []
        self.sem = Sem("q" + name)


class Buf:
    def __init__(self, name):
        self.name = name
        self.w = {}
        self.r = {}
        self.dsem = Sem("d" + name)


def merge(a, b):
    for k, v in b.items():
        if a.get(k, 0) < v:
            a[k] = v


class Sched:
    def __init__(self):
        self.pe = Q("pe")
        self.act = Q("act")
        self.dve = Q("dve")
        self.pool = Q("pool")
        self.sp = Q("sp")
        self.queues = [self.pe, self.act, self.dve, self.pool, self.sp]
        self.bufs = []
        self.extra_sems = []

    def buf(self, name):
        b = Buf(name)
        self.bufs.append(b)
        return b

    def op(self, q, fn, reads=(), writes=(), kind="c", dsem=None, merge_w=False, extra=None, no_waw=False):
        waits = {}
        for b in reads:
            merge(waits, b.w)
        for b in writes:
            merge(waits, b.r)
            if not no_waw:
                merge(waits, b.w)
        if extra:
            merge(waits, extra)
        if q is self.pe:
            waits.pop(self.pe.sem, None)
        if kind == "c":
            q.sem.val += 1
            tok = {q.sem: q.sem.val}
            sig = (q.sem, 1)
        elif kind == "dma":
            dsem.val += 16
            tok = {dsem: dsem.val}
            sig = (dsem, 16)
        else:
            dsem.val += 1
            tok = {dsem: dsem.val}
            sig = (dsem, 1)
        for b in writes:
            if merge_w:
                merge(b.w, tok)
            else:
                b.w = dict(tok)
            if not no_waw:
                b.r = {}
        for b in reads:
            merge(b.r, tok)
        q.ops.append((fn, waits, sig))
        return tok

    def all_tokens(self):
        t = {}
        for q in self.queues:
            if q.sem.val:
                t[q.sem] = q.sem.val
        for b in self.bufs:
            if b.dsem.val:
                t[b.dsem] = b.dsem.val
        for s in self.extra_sems:
            if s.val:
                t[s] = s.val
        return t

    def barrier(self, extra=None, exclude=()):
        t = self.all_tokens()
        for sem in exclude:
            t.pop(sem, None)
        if extra:
            merge(t, extra)
        for q in self.queues:
            w = dict(t)
            q.ops.append((None, w, None))

    def sems(self):
        out = [q.sem for q in self.queues]
        out += [b.dsem for b in self.bufs if b.dsem.val]
        out += [s for s in self.extra_sems]
        return out


def I(name, *args, **kw):
    return (name, args, kw)


def replay(q, eng):
    seen = {}
    for fn, waits, sig in q.ops:
        for sem, val in waits.items():
            if seen.get(sem, 0) < val:
                eng.wait_ge(sem.h, val)
                seen[sem] = val
        if fn is None:
            continue
        if callable(fn):
            ins = fn(eng)
        else:
            ins = getattr(eng, fn[0])(*fn[1], **fn[2])
        if sig is not None:
            ins.then_inc(sig[0].h, sig[1])


def build_program(debug=False, stop_after=None):
    nc = bass.Bass("TRN2", target_bir_lowering=False)
    sc = Sched()
    PE, ACT, DVE, POOL, SP = sc.pe, sc.act, sc.dve, sc.pool, sc.sp

    def ext_in(name, shape, dt=F32):
        return nc.dram_tensor(name, shape, dt, kind="ExternalInput")

    xT_d = ext_in("xT", [128, 8, HALF])
    pT_d = ext_in("pT", [DEPTH, 128, 2, HALF])
    win_d = ext_in("win", [DEPTH, 128, 8, 2048])
    wout_d = ext_in("wout", [DEPTH, 128, 8, 1024])
    wpg_d = ext_in("wpg", [DEPTH, 128, 8, 1024])
    wpe_d = ext_in("wpe", [DEPTH, 128, 2, 1024])
    NV = 8 + 8 + 8 + 6 + 2 + 2 + 2
    vec_d = ext_in("vec", [128, DEPTH * NV + 8])
    cst_d = ext_in("cst", [128, 6 * 128])
    outT_d = nc.dram_tensor("outT", [128, 8, HALF], F32, kind="ExternalOutput")

    hb_d = [nc.dram_tensor(f"hb{i}", [D, 512], BF16) for i in range(4)]
    hg_d = [nc.dram_tensor(f"hg{i}", [2 * D, 512], BF16) for i in range(4)]
    YSH = [(256, S), (128, S)] + [(128, 1024)] * 4
    NY = len(YSH)
    yb_d = [nc.dram_tensor(f"yb{i}", [YSH[i][0], YSH[i][1]], BF16) for i in range(NY)]
    yg_d = [nc.dram_tensor(f"yg{i}", [2 * YSH[i][0], YSH[i][1]], BF16) for i in range(NY)]
    gz_d = nc.dram_tensor("gz", [256, S], BF16)
    dbg = {}
    if debug:
        dbg["hg"] = [nc.dram_tensor(f"dbg_hg{i}", [2 * D, 512], BF16, kind="ExternalOutput") for i in range(4)]
        dbg["yg"] = [nc.dram_tensor(f"dbg_yg{i}", [2 * YSH[i][0], YSH[i][1]], BF16, kind="ExternalOutput")
                     for i in range(NY)]

    hb_b = [sc.buf(f"hb{i}") for i in range(4)]
    hg_b = [sc.buf(f"hg{i}") for i in range(4)]
    yb_b = [sc.buf(f"yb{i}") for i in range(NY)]
    yg_b = [sc.buf(f"yg{i}") for i in range(NY)]
    gz_b = sc.buf("gz")
    out_b = sc.buf("outd")
    cc_sem = Sem("cc")
    sc.extra_sems.append(cc_sem)

    with contextlib.ExitStack() as es:
        def sb(name, shape, dt):
            return es.enter_context(nc.sbuf_tensor(name, shape, dt))

        def ps(name):
            return es.enter_context(nc.psum_tensor(name, [128, 512], F32))

        xT = sb("xT_s", [128, 8, HALF], F32)
        wbuf = sb("wbuf", [128, 18432], BF16)
        NWS = 4
        wst = [sb(f"wst{i}", [128, 512], F32) for i in range(NWS)]
        big = sb("bigbf", [128, 24576], BF16)
        bfa = sb("bfa", [128, 24, 512], BF16)
        NF = 12
        FA = sb("FA", [128, NF * 514], F32)
        Fp = [FA[:, i * 514:(i + 1) * 514] for i in range(NF)]

        def Fpair(i):
            return FA[:, i * 514:(i + 2) * 514].rearrange("p (a c) -> p a c", a=2)
        NCST = 6 * 128
        cbf = sb("cbf", [128, NCST], BF16)
        vec = sb("vecs", [128, DEPTH * NV + 8], F32)
        nbpg = sb("nbpg", [128, DEPTH * 8 + 1], F32)
        epscol = nbpg[:, DEPTH * 8:DEPTH * 8 + 1]
        P2 = [es.enter_context(nc.psum_tensor(f"pp{i}", [128, 1024], F32)) for i in range(4)]
        P = [P2[i // 2][:, (i % 2) * 512:(i % 2 + 1) * 512] for i in range(8)]

        def Ppair(i):
            return P2[i][:, :].rearrange("p (a c) -> p a c", a=2)
        hTt = [bfa[:, 0:8, :], bfa[:, 8:16, :]]
        Bp = [bfa[:, 16 + i, :] for i in range(8)]
        Bx = [bfa[:, i, :] for i in range(8)]

        b_xT = [sc.buf(f"xT{t}") for t in range(4)]
        b_wlo = sc.buf("wbuf_lo")
        b_whi = sc.buf("wbuf_hi")
        b_wst = [sc.buf(f"wst{i}") for i in range(NWS)]
        b_hTt = [sc.buf(f"hTt{i}") for i in range(2)]
        b_F = [sc.buf(f"F{i}") for i in range(NF)]
        b_B = [sc.buf(f"B{i}") for i in range(8)]
        b_Bx = [sc.buf(f"Bx{i}") for i in range(8)]
        b_P = [sc.buf(f"P{i}") for i in range(8)]
        b_cst = sc.buf("cst")
        b_vec = sc.buf("vec")
        b_q = sc.buf("qT")
        b_k = sc.buf("kT")
        b_v = sc.buf("V")
        b_yT = [sc.buf(f"yT{i}") for i in range(2)]
        b_hp = sc.buf("hp")
        b_pTb = sc.buf("pTb")
        b_sq2 = sc.buf("sq2")
        b_hp2 = sc.buf("hp2")
        b_swst = sc.buf("swst")

        qT = big[:, 0:8192].rearrange("p (a t) -> p a t", a=2)
        kT = big[:, 8192:16384].rearrange("p (a t) -> p a t", a=2)
        V = big[:, 16384:24576].rearrange("p (n c) -> p n c", c=256)
        yTt = [big[:, i * 4096:(i + 1) * 4096].rearrange("p (c t) -> p c t", c=8) for i in range(2)]
        hp = big[:, 8192:12288].rearrange("p (c t) -> p c t", c=8)
        pTb = big[:, 16384:17408].rearrange("p (c t) -> p c t", c=2)

        triN = cbf[:, 0:128]
        triC = cbf[:, 128:256]
        ident = cbf[:, 256:384]
        gsum = cbf[:, 384:512]
        onesms = cbf[:, 512:640]
        mask01 = cbf[:, 640:768]

        def vcol(layer, off, n=1):
            base = layer * NV + off
            return vec[:, base:base + n]
        V_NORMG, V_PLEN, V_BPG, V_CW, V_CB, V_BGC, V_BGA = 0, 8, 16, 24, 30, 32, 34
        fgcol = lambda k: vec[:, DEPTH * NV + k: DEPTH * NV + k + 1]

        sc.op(SP, I("dma_start", out=vec[:], in_=vec_d[:, :]), writes=[b_vec], kind="dma", dsem=b_vec.dsem)
        csrc = [(0, 512), (512, 768)]
        for i, (a, b) in enumerate(csrc):
            n = b - a
            sc.op(SP, I("dma_start", out=Fp[i][:, 0:n], in_=cst_d[:, a:b]),
                  writes=[b_F[i]], kind="dma", dsem=b_F[i].dsem)
            sc.op(DVE, I("tensor_copy", out=cbf[:, a:b], in_=Fp[i][:, 0:n]),
                  reads=[b_F[i]], writes=[b_cst], merge_w=True)
        xtok = None
        for t4 in range(4):
            xtok = sc.op(SP, I("dma_start", out=xT[:, :, t4 * 512:(t4 + 1) * 512],
                               in_=xT_d[:, :, t4 * 512:(t4 + 1) * 512]),
                         writes=[b_xT[t4]], kind="dma", dsem=b_xT[t4].dsem, extra=xtok)
        for l_ in range(DEPTH):
            sc.op(DVE, I("tensor_scalar", out=nbpg[:, l_ * 8:(l_ + 1) * 8], in0=vcol(l_, V_BPG, 8), scalar1=-1.0,
                         scalar2=None, op0=ALU.mult), reads=[b_vec], writes=[b_vec], merge_w=True)
        sc.op(DVE, I("memset", epscol, EPS), reads=[b_vec], writes=[b_vec], merge_w=True)

        wst_i = [0]

        def w_piece(src_ap, dst_off, scale_col, eng, wide=False, fslots=None, defer=None):
            fsl = fslots if fslots is not None else (list(range(2, 12)) if wide else [])
            nsl = NWS + len(fsl)
            sl = wst_i[0] % nsl
            wst_i[0] += 1
            if sl < NWS:
                stg, stb = wst[sl][:, :], b_wst[sl]
            else:
                stg, stb = Fp[fsl[sl - NWS]][:, 0:512], b_F[fsl[sl - NWS]]
            sc.op(SP, I("dma_start", out=stg, in_=src_ap), writes=[stb], kind="dma", dsem=stb.dsem)
            dst = wbuf[:, dst_off:dst_off + 512]
            rd = [stb] + ([b_vec] if scale_col is not None else [])
            if eng is ACT:
                if scale_col is None:
                    ins = I("copy", out=dst, in_=stg)
                else:
                    ins = I("mul", out=dst, in_=stg, mul=scale_col)
            else:
                if scale_col is None:
                    ins = I("tensor_copy", out=dst, in_=stg)
                else:
                    ins = I("tensor_scalar", out=dst, in0=stg, scalar1=scale_col, scalar2=None,
                            op0=ALU.mult)
            def cast():
                sc.op(eng, ins, reads=rd, writes=[b_wlo if dst_off < 8192 else b_whi], merge_w=True, no_waw=True)
            if defer is None:
                cast()
            else:
                defer.append(cast)

        def win_pieces(layer):
            out = []
            for k in range(8):
                for c0 in range(0, 2048, 512):
                    out.append((win_d[layer, :, k, c0:c0 + 512], k * 2048 + c0, vcol(layer, V_NORMG + k)))
            return out

        def wd_pieces(layer):
            out = []
            for k in range(8):
                for c0 in range(0, 1024, 512):
                    out.append((wout_d[layer, :, k, c0:c0 + 512], k * 1024 + c0, None))
            for k in range(8):
                for c0 in range(0, 1024, 512):
                    out.append((wpg_d[layer, :, k, c0:c0 + 512], 8192 + k * 1024 + c0, vcol(layer, V_PLEN + k)))
            for j in range(2):
                for c0 in range(0, 1024, 512):
                    out.append((wpe_d[layer, :, j, c0:c0 + 512], 16384 + j * 1024 + c0, None))
            return out

        def load_win(layer, lo=0, hi=32, engs=None):
            engs = engs or [ACT, DVE]
            for i, (src, off, scol) in enumerate(win_pieces(layer)):
                if lo <= i < hi:
                    w_piece(src, off, scol, engs[i % len(engs)], wide=True)

        win = wbuf[:, 0:16384].rearrange("p (k c) -> p k c", k=8)
        wout = wbuf[:, 0:8192].rearrange("p (k c) -> p k c", k=8)
        wpg = wbuf[:, 8192:16384].rearrange("p (k c) -> p k c", k=8)
        wpe = wbuf[:, 16384:18432].rearrange("p (k c) -> p k c", k=2)

        half_cache = {}

        def rstd_from(psb, pst, fb, f):
            sc.op(ACT, I("activation", out=f[:, 0:512], in_=pst[:, :], func=AF.Ln, bias=epscol),
                  reads=[psb, b_vec], writes=[fb])
            sc.op(ACT, I("activation", out=f[:, 0:512], in_=f[:, 0:512], func=AF.Exp, scale=-0.5),
                  reads=[fb], writes=[fb])

        def sigmoid_act(pst, psb, f, fb, nbias=None):
            kw = {} if nbias is None else {"bias": nbias}
            rd = [psb] + ([] if nbias is None else [b_vec])
            sc.op(ACT, I("activation", out=f[:, 0:512], in_=pst[:, :], func=AF.Exp, scale=-1.0, **kw),
                  reads=rd, writes=[fb])
            sc.op(ACT, I("activation", out=f[:, 0:512], in_=f[:, 0:512], func=AF.Ln, bias=1.0),
                  reads=[fb], writes=[fb])
            sc.op(ACT, I("activation", out=f[:, 0:512], in_=f[:, 0:512], func=AF.Exp, scale=-1.0),
                  reads=[fb], writes=[fb])

        def norm_sq(t4, sq, sq_b):
            t0 = t4 * 512
            merge(b_sq2.r, sq_b.r)
            merge(b_sq2.w, sq_b.w)
            for hf in range(2):
                hb_ = sq_b if hf == 0 else b_sq2
                sc.op(ACT, I("activation", out=sq[:, hf * 4:(hf + 1) * 4, :], in_=xT[:, hf * 4:(hf + 1) * 4, t0:t0 + 512],
                             func=AF.Square), reads=[b_xT[t4]], writes=[hb_])

        def norm_ms(sq, sq_b, psi):
            for k in range(8):
                hb_ = sq_b if k < 4 else b_sq2
                tok = sc.op(PE, I("matmul", P[psi][:, :], lhsT=onesms, rhs=sq[:, k, :], start=(k == 0),
                                  stop=(k == 7)), reads=[hb_, b_cst], writes=[b_P[psi]])
                if k >= 4:
                    merge(sq_b.r, tok)
            merge(sq_b.w, b_sq2.w)

        def norm_front(t4, sq, sq_b, psi, fi):
            norm_sq(t4, sq, sq_b)
            norm_ms(sq, sq_b, psi)
            rstd_from(b_P[psi], P[psi], b_F[fi], Fp[fi])

        def norm_back(t4, dst, dst_b, fi):
            t0 = t4 * 512
            for k in range(8):
                sc.op(DVE, I("tensor_tensor", out=dst[:, k, :], in0=xT[:, k, t0:t0 + 512],
                             in1=Fp[fi][:, 0:512], op=ALU.mult),
                      reads=[b_xT[t4], b_F[fi]], writes=[dst_b], merge_w=(k > 0), no_waw=(k > 0))

        def norm_tile(t4, dst, dst_b, sq, sq_b, psi, fi):
            norm_front(t4, sq, sq_b, psi, fi)
            norm_back(t4, dst, dst_b, fi)

        def gather(src_d, src_b, dst_d, dst_b):
            sc.op(POOL, I("collective_compute",
                          "AllGather", ALU.bypass, replica_groups=[[0, 1], [2, 3], [4, 5], [6, 7]],
                          ins=[src_d.ap().opt()], outs=[dst_d.ap().opt()]),
                  reads=[src_b], writes=[dst_b], kind="cc", dsem=dst_b.dsem)

        def phase_A_back(t4, stq=None):
            norm_back(t4, hTt[1], b_hTt[1], 11)
            sc.op(stq or SP, I("dma_start", out=hb_d[t4][:, :].rearrange("(k p) t -> p k t", p=128), in_=hTt[1][:, :, :]),
                  reads=[b_hTt[1]], writes=[hb_b[t4]], kind="dma",
                  dsem=(b_swst.dsem if stq is POOL else b_hTt[1].dsem))
            gather(hb_d[t4], hb_b[t4], hg_d[t4], hg_b[t4])

        def phase_A_tile(t4, stq=None):
            norm_front(t4, hTt[0], b_hTt[0], 7, 11)
            phase_A_back(t4, stq)

        def load_y(t4, extra=None):
            ys = t4 % 2
            yT = yTt[ys]
            t0 = t4 * 512
            plan = [(0, 0, 0, 2), (0, 1, 2, 2), (1, 0, 4, 1), (1, 1, 6, 1), (2 + t4, 0, 5, 1), (2 + t4, 1, 7, 1)]
            for pi, (part, r, c0, nch) in enumerate(plan):
                def ld(e, part=part, r=r, c0=c0, nch=nch, yT=yT, t0=t0):
                    if "half_sp" not in half_cache:
                        half_cache["half_sp"] = e.snap(e.partition_id() % 2, min_val=0, max_val=1)
                    half = half_cache["half_sp"]
                    if part < 2:
                        ygv = yg_d[part].ap().rearrange("(r c p) (hh t) -> r hh p c t", r=2, c=nch, p=128, hh=2)
                        src = ygv[r, bass.ds(half, 1), :, :, t0:t0 + 512]
                    else:
                        ygv = yg_d[part].ap().rearrange("(r c p) (hh t) -> r hh p c t", r=2, c=1, p=128, hh=2)
                        src = ygv[r, bass.ds(half, 1), :, :, :]
                    return e.dma_start(out=yT[:, c0:c0 + nch, :], in_=src)
                sc.op(SP, ld, reads=[yg_b[part]], writes=[b_yT[ys]], kind="dma", dsem=b_yT[ys].dsem,
                      merge_w=(pi > 0), extra=extra)


        def phase_B(layer):
            for cc in range(2):
                sc.op(POOL, I("memset", Fp[6 + cc][:, 0:2], 0.0), writes=[b_F[6 + cc]])
            rot = [0]

            def load_hT(tt):
                hs_, half_ = tt % 2, tt // 4
                sc.op(SP, I("dma_start", out=hTt[hs_][:, :, :],
                            in_=hg_d[tt % 4][half_ * D:(half_ + 1) * D, :].rearrange("(k p) t -> p k t", p=128)),
                      reads=[hg_b[tt % 4]], writes=[b_hTt[hs_]], kind="dma", dsem=b_hTt[hs_].dsem)

            for tt in range(8):
                hs = tt % 2
                hT = hTt[hs]
                half, tl = tt // 4, (tt % 4) * 512
                tok0 = tt * 512
                if tt == 0:
                    load_hT(0)

                def proj(psi, col0):
                    for k in range(8):
                        sc.op(PE, I("matmul", P[psi][:, :], lhsT=win[:, k, col0:col0 + 128],
                                    rhs=hT[:, k, :], start=(k == 0), stop=(k == 7)),
                              reads=[b_hTt[hs], b_wlo, b_whi], writes=[b_P[psi]])

                def nextrot():
                    psi = 5 + rot[0] % 3
                    rot[0] += 1
                    return psi

                def conv_front(cc):
                    for si in range(4):
                        proj(si, si * 256 + cc * 128)
                    u, ub = Fp[6 + cc], b_F[6 + cc]
                    w_ = lambda j: vcol(layer, V_CW + cc * 3 + j)
                    sc.op(ACT, I("activation", out=Fp[0][:, 0:512], in_=P[1][:, :], func=AF.Copy),
                          reads=[b_P[1]], writes=[b_F[0]])
                    sigmoid_act(P[3], b_P[3], Fp[5], b_F[5])
                    sc.op(DVE, I("tensor_tensor", out=u[:, 2:514], in0=Fp[0][:, 0:512], in1=P[2][:, :], op=ALU.mult),
                          reads=[b_F[0], b_P[2]], writes=[ub], merge_w=True)
                    sc.op(DVE, I("tensor_scalar", out=Fp[1][:, 0:512], in0=u[:, 2:514], scalar1=w_(2),
                                 scalar2=vcol(layer, V_CB + cc), op0=ALU.mult, op1=ALU.add),
                          reads=[ub, b_vec], writes=[b_F[1]])
                    sc.op(DVE, I("scalar_tensor_tensor", out=Fp[2][:, 0:512], in0=u[:, 1:513], scalar=w_(1),
                                 in1=Fp[1][:, 0:512], op0=ALU.mult, op1=ALU.add),
                          reads=[ub, b_vec, b_F[1]], writes=[b_F[2]])
                    sc.op(DVE, I("scalar_tensor_tensor", out=Fp[1][:, 0:512], in0=u[:, 0:512], scalar=w_(0),
                                 in1=Fp[2][:, 0:512], op0=ALU.mult, op1=ALU.add),
                          reads=[ub, b_vec, b_F[2]], writes=[b_F[1]])
                    sc.op(POOL, I("tensor_copy", out=u[:, 0:2], in_=u[:, 512:514]), reads=[ub], writes=[ub])
                    sc.op(DVE, I("tensor_tensor", out=Fp[3][:, 0:512], in0=Fp[1][:, 0:512], in1=P[0][:, :],
                                 op=ALU.mult), reads=[b_F[1], b_P[0]], writes=[b_F[3]])
                    sc.op(POOL, I("tensor_tensor", out=Bp[0][:, :], in0=Fp[3][:, 0:512], in1=Fp[3][:, 0:512],
                                  op=ALU.mult), reads=[b_F[3]], writes=[b_B[0]])
                    sc.op(DVE, I("tensor_tensor", out=Fp[2][:, 0:512], in0=P[3][:, :], in1=Fp[5][:, 0:512],
                                 op=ALU.mult), reads=[b_P[3], b_F[5]], writes=[b_F[2]])

                def conv_back(cc):
                    sc.op(PE, I("matmul", P[4][:, :], lhsT=gsum, rhs=Bp[0][:, :], start=True, stop=True),
                          reads=[b_B[0], b_cst], writes=[b_P[4]])
                    rstd_from(b_P[4], P[4], b_F[4], Fp[4])
                    sc.op(DVE, I("tensor_tensor", out=Fp[3][:, 0:512], in0=Fp[3][:, 0:512], in1=Fp[4][:, 0:512],
                                 op=ALU.mult), reads=[b_F[3], b_F[4]], writes=[b_F[3]])
                    ob = 1 + cc
                    sc.op(DVE, I("scalar_tensor_tensor", out=Bp[ob][:, :], in0=Fp[3][:, 0:512],
                                 scalar=vcol(layer, V_BGC + cc), in1=Fp[2][:, 0:512], op0=ALU.mult, op1=ALU.mult),
                          reads=[b_F[3], b_F[2], b_vec], writes=[b_B[ob]])
                    sc.op(SP, I("dma_start", out=yb_d[0][cc * 128:(cc + 1) * 128, tok0:tok0 + 512], in_=Bp[ob][:, :]),
                          reads=[b_B[ob]], writes=[yb_b[0]], kind="dma", dsem=b_B[ob].dsem, merge_w=True)

                def qk_part():
                    for pair in range(2):
                        psi = nextrot()
                        proj(psi, 1024 + pair * 128)
                        sc.op(ACT, I("activation", out=qT[:, pair, tok0:tok0 + 512], in_=P[psi][:, :], func=AF.Copy,
                                     scale=0.125), reads=[b_P[psi]], writes=[b_q], merge_w=True)
                        psi = nextrot()
                        proj(psi, 1280 + pair * 128)
                        sc.op(ACT, I("activation", out=kT[:, pair, tok0:tok0 + 512], in_=P[psi][:, :], func=AF.Copy),
                              reads=[b_P[psi]], writes=[b_k], merge_w=True)

                def azv_part():
                    for bp in range(2):
                        psi = nextrot()
                        for bl in range(2):
                            blk = bp * 2 + bl
                            for k in range(8):
                                sc.op(PE, I("matmul", P[psi][:, bl * 256:(bl + 1) * 256],
                                            lhsT=hT[:, k, blk * 128:(blk + 1) * 128],
                                            rhs=win[:, k, 1536:1792], start=(k == 0), stop=(k == 7)),
                                      reads=[b_hTt[hs], b_wlo, b_whi], writes=[b_P[psi]])
                        sc.op(ACT, I("activation", out=V[:, tt * 4 + bp * 2: tt * 4 + bp * 2 + 2, :],
                                     in_=P[psi][:, :].rearrange("p (n c) -> p n c", c=256), func=AF.Copy),
                              reads=[b_P[psi]], writes=[b_v], merge_w=True)
                    for pair in range(2):
                        psi = nextrot()
                        proj(psi, 1792 + pair * 128)
                        gb = 3 + pair
                        fe, fe_b = Fp[8 + pair], b_F[8 + pair]
                        sigmoid_act(P[psi], b_P[psi], fe, fe_b)
                        sc.op(DVE, I("tensor_tensor", out=Bp[gb][:, :], in0=P[psi][:, :], in1=fe[:, 0:512],
                                     op=ALU.mult), reads=[b_P[psi], fe_b], writes=[b_B[gb]])
                        sc.op(SP, I("dma_start", out=gz_d[pair * 128:(pair + 1) * 128, tok0:tok0 + 512],
                                    in_=Bp[gb][:, :]),
                              reads=[b_B[gb]], writes=[gz_b], kind="dma", dsem=b_B[gb].dsem, merge_w=True)

                conv_front(0)
                if tt + 1 < 8:
                    load_hT(tt + 1)
                qk_part()
                conv_back(0)
                conv_front(1)
                azv_part()
                conv_back(1)

        def phase_C(layer, wd_list):
            for i in range(8):
                merge(b_Bx[i].r, b_hTt[0].r)
                merge(b_Bx[i].r, b_hTt[0].w)
                merge(b_Bx[i].w, b_hTt[0].w)
            streams = []
            for pair in range(2):
                for qi in range(8):
                    nkb = 4 * (qi + 1)
                    streams.append([(pair, qi, kb, i == 0, kb == 0) for i, kb in enumerate(range(nkb - 1, -1, -1))])
            sblocks = []
            pre = [False] * len(streams)
            for i, st in enumerate(streams):
                pend = st[4:] if pre[i] else st
                nfull = len(st) - 4
                if i + 1 < len(streams) and nfull >= 4 and len(streams[i + 1]) > 4:
                    sblocks += pend[:-4]
                    for j in range(4):
                        sblocks.append(pend[len(pend) - 4 + j])
                        sblocks.append(streams[i + 1][j])
                    pre[i + 1] = True
                else:
                    sblocks += pend
            ns = len(sblocks)
            assert ns == sum(len(st) for st in streams)

            def cbp(s):
                return 2 if (sblocks[s][0] * 8 + sblocks[s][1]) % 2 == 0 else 1
            OB = [6, 7]
            eF = [[0, 2, 4]]
            wF = [[6, 8]]
            F_O, F_R = 10, 11
            Lb = [[(Bp[0], b_B[0]), (Bp[1], b_B[1])], [(Bp[2], b_B[2]), (Bp[3], b_B[3])]]
            Ab = [[(Bp[4], b_B[4]), (Bp[5], b_B[5])], [(Bp[6], b_B[6]), (Bp[7], b_B[7])]]
            SQ = (Bx[0], b_Bx[0])
            GT = [(Bx[1], b_Bx[1]), (Bx[2], b_Bx[2])]
            YA = [(Bx[3], b_Bx[3]), (Bx[4], b_Bx[4])]

            def c0of(s):
                pair, qi, kb, first, last = sblocks[s]
                r = kb - 4 * qi
                return r * 128 if r > 0 else 0

            def qk(s, hh):
                pair, qi, kb, first, last = sblocks[s]
                z = hh
                c0 = c0of(s)
                pr = slice(hh * 64, (hh + 1) * 64)
                sc.op(PE, I("matmul", P[z][:, c0:512], lhsT=kT[pr, pair, kb * 128:(kb + 1) * 128],
                            rhs=qT[pr, pair, qi * 512 + c0:(qi + 1) * 512], start=True, stop=True),
                      reads=[b_q, b_k], writes=[b_P[z]])

            def exp_z(s):
                c0 = c0of(s)
                f = eF[0][s % 3]
                sc.op(ACT, I("activation", out=Fpair(f)[:, :, c0:512], in_=Ppair(0)[:, :, c0:512], func=AF.Exp),
                      reads=[b_P[0], b_P[1]], writes=[b_F[f], b_F[f + 1]])
                pair_, qi_, kb_ = sblocks[s][0], sblocks[s][1], sblocks[s][2]
                if kb_ >= 4 * qi_:
                    for hh in range(2):
                        sc.op(DVE, I("tensor_tensor", out=Fp[f + hh][:, c0:c0 + 128], in0=Fp[f + hh][:, c0:c0 + 128],
                                     in1=mask01, op=ALU.mult),
                              reads=[b_F[f + hh], b_cst], writes=[b_F[f + hh]])

            def ln_l(s):
                c0 = c0of(s)
                f = eF[0][s % 3]
                li = 2 * (s % 2)
                sc.op(ACT, I("activation", out=bfa[:, 16 + li:16 + li + 2, c0:512], in_=Fpair(f)[:, :, c0:512],
                             func=AF.Ln, bias=1.0),
                      reads=[b_F[f], b_F[f + 1]], writes=[b_B[li], b_B[li + 1]])

            def tri(s, hh):
                first = sblocks[s][3]
                c0 = c0of(s)
                li = 2 * (s % 2) + hh
                cb = 2 * cbp(s) + hh
                sc.op(PE, I("matmul", P[cb][:, c0:512], lhsT=triN, rhs=Bp[li][:, c0:512], start=first, stop=True,
                            skip_group_check=True),
                      reads=[b_B[li], b_cst], writes=[b_P[cb]])

            def tric(s, hh):
                c0 = c0of(s)
                li = 2 * (s % 2) + hh
                cb = 2 * cbp(s) + hh
                sc.op(PE, I("matmul", P[cb][:, c0:512], lhsT=triC, rhs=Bp[li][:, c0:512], start=False, stop=True,
                            skip_group_check=True),
                      reads=[b_B[li], b_cst], writes=[b_P[cb]])

            def w_exp(s):
                c0 = c0of(s)
                f = wF[0][s % 2]
                cp_ = cbp(s)
                sc.op(ACT, I("activation", out=Fpair(f)[:, :, c0:512], in_=Ppair(cp_)[:, :, c0:512], func=AF.Exp),
                      reads=[b_P[2 * cp_], b_P[2 * cp_ + 1]], writes=[b_F[f], b_F[f + 1]])

            def mul_a(s):
                c0 = c0of(s)
                fe, fw = eF[0][s % 3], wF[0][s % 2]
                ai = 4 + 2 * (s % 2)
                sc.op(DVE, I("tensor_tensor", out=bfa[:, 16 + ai:16 + ai + 2, c0:512], in0=Fpair(fe)[:, :, c0:512],
                             in1=Fpair(fw)[:, :, c0:512], op=ALU.mult),
                      reads=[b_F[fe], b_F[fe + 1], b_F[fw], b_F[fw + 1]], writes=[b_B[ai], b_B[ai + 1]])

            def av(s, hh):
                pair, qi, kb, first, last = sblocks[s]
                c0 = c0of(s)
                ai = 4 + 2 * (s % 2) + hh
                o = OB[(pair * 8 + qi) % 2]
                h4 = pair * 2 + hh
                sc.op(PE, I("matmul", P[o][hh * 64:(hh + 1) * 64, c0:512], lhsT=V[:, kb, h4 * 64:(h4 + 1) * 64],
                            rhs=Bp[ai][:, c0:512], start=first, stop=last, skip_group_check=True),
                      reads=[b_v, b_B[ai]], writes=[b_P[o]])

            def epilogue1(pair, qi):
                o = OB[(pair * 8 + qi) % 2]
                tok0 = qi * 512
                gt, gb = GT[(pair * 8 + qi) % 2]
                sc.op(SP, I("dma_start", out=gt[:, :], in_=gz_d[pair * 128:(pair + 1) * 128, tok0:tok0 + 512]),
                      reads=[gz_b], writes=[gb], kind="dma", dsem=gb.dsem)
                sc.op(DVE, I("tensor_copy", out=Fp[F_O][:, 0:512], in_=P[o][:, :]),
                      reads=[b_P[o]], writes=[b_F[F_O]])
                sc.op(DVE, I("tensor_tensor", out=SQ[0][:, :], in0=Fp[F_O][:, 0:512], in1=Fp[F_O][:, 0:512],
                             op=ALU.mult), reads=[b_F[F_O]], writes=[SQ[1]])

            def epilogue1b(pair, qi):
                o = OB[(pair * 8 + qi) % 2]
                sc.op(PE, I("matmul", P[o][:, :], lhsT=gsum, rhs=SQ[0][:, :], start=True, stop=True),
                      reads=[SQ[1], b_cst], writes=[b_P[o]])

            def epilogue2(pair, qi):
                o = OB[(pair * 8 + qi) % 2]
                tok0 = qi * 512
                gt, gb = GT[(pair * 8 + qi) % 2]
                yt, yb_ = YA[(pair * 8 + qi) % 2]
                rstd_from(b_P[o], P[o], b_F[F_R], Fp[F_R])
                sc.op(DVE, I("tensor_tensor", out=Fp[F_O][:, 0:512], in0=Fp[F_O][:, 0:512], in1=Fp[F_R][:, 0:512],
                             op=ALU.mult), reads=[b_F[F_O], b_F[F_R]], writes=[b_F[F_O]])
                sc.op(DVE, I("scalar_tensor_tensor", out=yt[:, :], in0=Fp[F_O][:, 0:512],
                             scalar=vcol(layer, V_BGA + pair), in1=gt[:, :], op0=ALU.mult, op1=ALU.mult),
                      reads=[b_F[F_O], gb, b_vec], writes=[yb_])
                if pair == 0:
                    part, dst = 1, yb_d[1][:, tok0:tok0 + 512]
                else:
                    part = 2 + qi % 4
                    dst = yb_d[part][:, (qi // 4) * 512:(qi // 4 + 1) * 512]
                sc.op(SP, I("dma_start", out=dst, in_=yt[:, :]),
                      reads=[yb_], writes=[yb_b[part]], kind="dma", dsem=yb_.dsem, merge_w=True)
                if pair == 0 and qi == 7:
                    gather(yb_d[1], yb_b[1], yg_d[1], yg_b[1])
                if pair == 1 and qi >= 4:
                    gather(yb_d[part], yb_b[part], yg_d[part], yg_b[part])
                if pair == 1 and qi == 5:
                    load_y(0, extra=dict(b_q.r))

            wd_it = iter(wd_list)
            pending = []
            for hh in range(2):
                qk(0, hh)
            for s in range(ns + 1):
                if s < ns:
                    exp_z(s)
                if s + 1 < ns:
                    for hh in range(2):
                        qk(s + 1, hh)
                if s >= 1:
                    w_exp(s - 1)
                if s < ns:
                    ln_l(s)
                if s >= 1 and not sblocks[s - 1][4]:
                    for hh in range(2):
                        tric(s - 1, hh)
                if s < ns:
                    for hh in range(2):
                        tri(s, hh)
                if s >= 1:
                    mul_a(s - 1)
                    for hh in range(2):
                        av(s - 1, hh)
                    if sblocks[s - 1][4]:
                        epilogue1(sblocks[s - 1][0], sblocks[s - 1][1])
                        pending.append((s + 1, 0, sblocks[s - 1][0], sblocks[s - 1][1]))
                        pending.append((s + 3, 1, sblocks[s - 1][0], sblocks[s - 1][1]))
                        pending.sort()
                while pending and pending[0][0] <= s:
                    _, st_, p_, q_ = pending.pop(0)
                    (epilogue1b if st_ == 0 else epilogue2)(p_, q_)
                if s % 6 == 3:
                    nxt = next(wd_it, None)
                    if nxt is not None:
                        w_piece(nxt[0], nxt[1], nxt[2], DVE)
            for _, st_, p_, q_ in sorted(pending):
                (epilogue1b if st_ == 0 else epilogue2)(p_, q_)
            for nxt in wd_it:
                w_piece(nxt[0], nxt[1], nxt[2], DVE)

        def phase_D(layer):
            last_layer = layer == DEPTH - 1
            pTbs = [big[:, 16384 + i * 1024: 16384 + (i + 1) * 1024].rearrange("p (c t) -> p c t", c=2)
                    for i in range(2)]
            b_pT2 = [b_pTb, b_v]

            def load_p(t4):
                t0 = t4 * 512
                for j in range(2):
                    sc.op(SP, I("dma_start", out=Fp[j][:, 0:512], in_=pT_d[layer, :, j, t0:t0 + 512]),
                          writes=[b_F[j]], kind="dma", dsem=b_F[j].dsem)
                    sc.op(ACT, I("activation", out=pTbs[t4 % 2][:, j, :], in_=Fp[j][:, 0:512], func=AF.Copy),
                          reads=[b_F[j]], writes=[b_pT2[t4 % 2]], merge_w=(j > 0))

            def outproj(t4, n0, n1):
                t0 = t4 * 512
                ys = t4 % 2
                yT = yTt[ys]
                for n in range(n0, n1):
                    psi = n % 2
                    for c in range(8):
                        sc.op(PE, I("matmul", P[psi][:, :], lhsT=wout[:, c, n * 128:(n + 1) * 128], rhs=yT[:, c, :],
                                    start=(c == 0), stop=(c == 7)), reads=[b_yT[ys], b_wlo], writes=[b_P[psi]])
                    sc.op(DVE, I("tensor_tensor", out=xT[:, n, t0:t0 + 512], in0=xT[:, n, t0:t0 + 512],
                                 in1=P[psi][:, :], op=ALU.add),
                          reads=[b_P[psi], b_xT[t4]], writes=[b_xT[t4]])

            hps = [hp, big[:, 12288:16384].rearrange("p (c t) -> p c t", c=8)]
            b_hps = [b_hp, b_hp2]

            def gate(t4, n0=0, n1=8):
                t0 = t4 * 512
                pTb_ = pTbs[t4 % 2]
                hp_, b_hp_ = hps[t4 % 2], b_hps[t4 % 2]
                for n in range(n0, n1):
                    pg, pe_ = 2 + (n % 2) * 2, 3 + (n % 2) * 2
                    for k in range(8):
                        sc.op(PE, I("matmul", P[pg][:, :], lhsT=wpg[:, k, n * 128:(n + 1) * 128], rhs=hp_[:, k, :],
                                    start=(k == 0), stop=(k == 7)), reads=[b_hp_, b_whi], writes=[b_P[pg]])
                    for j in range(2):
                        sc.op(PE, I("matmul", P[pe_][:, :], lhsT=wpe[:, j, n * 128:(n + 1) * 128], rhs=pTb_[:, j, :],
                                    start=(j == 0), stop=(j == 1)), reads=[b_pT2[t4 % 2], b_whi], writes=[b_P[pe_]])
                    f = 2 + n % 2
                    sigmoid_act(P[pg], b_P[pg], Fp[f], b_F[f], nbias=nbpg[:, layer * 8 + n: layer * 8 + n + 1])
                    sc.op(DVE, I("tensor_tensor", out=Fp[f][:, 0:512], in0=Fp[f][:, 0:512], in1=P[pe_][:, :],
                                 op=ALU.mult), reads=[b_F[f], b_P[pe_]], writes=[b_F[f]])
                    sc.op(DVE, I("tensor_tensor", out=xT[:, n, t0:t0 + 512], in0=xT[:, n, t0:t0 + 512],
                                 in1=Fp[f][:, 0:512], op=ALU.add),
                          reads=[b_F[f], b_xT[t4]], writes=[b_xT[t4]])

            def final_back(t4):
                t0 = t4 * 512
                for k in range(8):
                    sc.op(DVE, I("scalar_tensor_tensor", out=xT[:, k, t0:t0 + 512], in0=xT[:, k, t0:t0 + 512],
                                 scalar=fgcol(k), in1=Fp[11][:, 0:512], op0=ALU.mult, op1=ALU.mult),
                          reads=[b_xT[t4], b_F[11], b_vec], writes=[b_xT[t4]])
                for hf in range(2):
                    sc.op(SP, I("dma_start", out=outT_d[:, hf * 4:(hf + 1) * 4, t0:t0 + 512],
                                in_=xT[:, hf * 4:(hf + 1) * 4, t0:t0 + 512]),
                          reads=[b_xT[t4]], writes=[out_b], kind="dma", dsem=out_b.dsem, merge_w=True)

            ew = {"on": False, "i": 0}
            late_casts = []
            nxt_pieces = win_pieces(layer + 1) if not last_layer else []

            def early_w(n, fslots=None, limit=16, defer=None):
                if not ew["on"]:
                    return
                for _ in range(n):
                    if ew["i"] < min(limit, len(nxt_pieces)):
                        src, off, scol = nxt_pieces[ew["i"]]
                        w_piece(src, off, scol, ACT if ew["i"] % 2 == 0 else DVE, fslots=fslots, defer=defer)
                        ew["i"] += 1

            load_p(0)
            load_y(1)
            outproj(0, 0, 8)
            load_p(1)
            load_y(2)
            norm_front(0, hTt[0], b_hTt[0], 7, 7)
            norm_back(0, hps[0], b_hps[0], 7)
            outproj(1, 0, 8)
            load_y(3)
            for t4 in range(4):
                nt = t4 + 1 < 4
                if nt:
                    norm_sq(t4 + 1, hTt[0], b_hTt[0])
                gate(t4, 0, 2)
                if nt:
                    norm_ms(hTt[0], b_hTt[0], 7)
                    rstd_from(b_P[7], P[7], b_F[7], Fp[7])
                gate(t4, 2, 4)
                early_w(4)
                if nt:
                    norm_back(t4 + 1, hps[(t4 + 1) % 2], b_hps[(t4 + 1) % 2], 7)
                gate(t4, 4, 8)
                early_w(4)
                if t4 == 3:
                    early_w(15, fslots=[4, 5, 6, 8, 9, 10, 7, 0, 1, 2, 3], limit=32, defer=late_casts)
                norm_sq(t4, hTt[0], b_hTt[0])
                if t4 + 2 < 4:
                    load_p(t4 + 2)
                    outproj(t4 + 2, 0, 2)
                norm_ms(hTt[0], b_hTt[0], 6)
                rstd_from(b_P[6], P[6], b_F[11], Fp[11])
                if t4 + 2 < 4:
                    outproj(t4 + 2, 2, 8)
                    if t4 + 2 == 3:
                        ew["on"] = True
                if not last_layer:
                    phase_A_back(t4)
                else:
                    final_back(t4)
            for c_ in late_casts:
                c_()
            return ew["i"]

        early_done = [0]

        class _Stop(Exception):
            pass
        nst = [0]

        def stage():
            nst[0] += 1
            if stop_after is not None and nst[0] > stop_after:
                raise _Stop()
        try:
            for t4 in range(4):
                phase_A_tile(t4, stq=POOL)
                load_win(0, t4 * 8, (t4 + 1) * 8)
            for layer in range(DEPTH):
                stage()
                xch = [b_.dsem for b_ in yg_b + hg_b]
                if layer > 0:
                    sc.barrier(exclude=xch)
                    load_win(layer, early_done[0], 32)
                stage()
                phase_B(layer)
                gather(yb_d[0], yb_b[0], yg_d[0], yg_b[0])
                stage()
                phase_C(layer, wd_pieces(layer))
                stage()
                sc.barrier(exclude=xch)
                stage()
                early_done[0] = phase_D(layer)
        except _Stop:
            pass
        if debug:
            sc.barrier()
            d1 = sc.buf("dbg1")
            for i in range(4):
                sc.op(SP, I("dma_start", out=dbg["hg"][i][:, :], in_=hg_d[i][:, :]), writes=[d1], kind="dma",
                      dsem=d1.dsem, merge_w=True)
            for i in range(NY):
                sc.op(SP, I("dma_start", out=dbg["yg"][i][:, :], in_=yg_d[i][:, :]), writes=[d1], kind="dma",
                      dsem=d1.dsem, merge_w=True)
        sc.barrier()

        for s_ in sc.sems():
            s_.h = es.enter_context(nc.semaphore(s_.name))
        with nc.Block() as block:
            @block.tensor
            def _(eng):
                replay(PE, eng)

            @block.scalar
            def _(eng):
                replay(ACT, eng)

            @block.vector
            def _(eng):
                replay(DVE, eng)

            @block.gpsimd
            def _(eng):
                replay(POOL, eng)

            @block.sync
            def _(eng):
                replay(SP, eng)
    return nc


def _chunk_rows(a):
    k = a.shape[0] // 128
    return np.ascontiguousarray(a.reshape(k, 128, a.shape[1]).transpose(1, 0, 2))


def _consts():
    j = np.arange(128)[:, None]
    s = np.arange(128)[None, :]
    triN = np.where(j >= s, -1.0, 0.0)
    triC = np.where(j < s, -1.0, 0.0)
    ident = np.eye(128)
    gsum = np.where((j // 64) == (s // 64), 1.0 / 64, 0.0)
    onesms = np.full((128, 128), 1.0 / 1024)
    mask01 = np.where(j < s, 1.0, 0.0)
    return np.ascontiguousarray(np.concatenate([triN, triC, ident, gsum, onesms, mask01], axis=1).astype(np.float32))


def prep_inputs(x, p, norm_g, w_in, conv_w, conv_b, branch_g, w_out, ple_norm_g, w_pg, b_pg, w_pe, final_g):
    f = lambda a: np.asarray(a, dtype=np.float32)
    x, p, norm_g, w_in, conv_w, conv_b, branch_g, w_out, ple_norm_g, w_pg, b_pg, w_pe, final_g = map(
        f, (x, p, norm_g, w_in, conv_w, conv_b, branch_g, w_out, ple_norm_g, w_pg, b_pg, w_pe, final_g))
    cst = _consts()
    per_g = []
    for g in range(2):
        cols = np.concatenate([np.arange(seg * 512 + g * 256, seg * 512 + (g + 1) * 256) for seg in range(8)])
        win = np.stack([_chunk_rows(w_in[i][:, cols]) for i in range(DEPTH)])
        vecs = []
        for i in range(DEPTH):
            cw = conv_w[i][:, g * 256:(g + 1) * 256].reshape(3, 2, 128).transpose(2, 1, 0).reshape(128, 6)
            v = np.concatenate([
                norm_g[i].reshape(8, 128).T, ple_norm_g[i].reshape(8, 128).T, b_pg[i].reshape(8, 128).T,
                cw, conv_b[i][g * 256:(g + 1) * 256].reshape(2, 128).T,
                branch_g[i][g * 256:(g + 1) * 256].reshape(2, 128).T,
                branch_g[i][512 + g * 256: 512 + (g + 1) * 256].reshape(2, 128).T], axis=1)
            vecs.append(v)
        vecs.append(final_g.reshape(8, 128).T)
        per_g.append((win, np.ascontiguousarray(np.concatenate(vecs, axis=1).astype(np.float32))))
    wout = np.stack([_chunk_rows(w_out[i]) for i in range(DEPTH)])
    wpg = np.stack([_chunk_rows(w_pg[i]) for i in range(DEPTH)])
    wpe = np.stack([_chunk_rows(w_pe[i]) for i in range(DEPTH)])
    maps = []
    for c in range(NCORES):
        b, g = c // 2, c % 2
        xT = _chunk_rows(np.ascontiguousarray(x[b, g * HALF:(g + 1) * HALF, :].T))
        pT = np.stack([_chunk_rows(np.ascontiguousarray(p[i, b, g * HALF:(g + 1) * HALF, :].T)) for i in range(DEPTH)])
        maps.append({"xT": xT, "pT": pT, "win": per_g[g][0], "wout": wout, "wpg": wpg, "wpe": wpe,
                     "vec": per_g[g][1], "cst": cst})
    return maps


def assemble(results):
    out = np.empty((4, S, D), np.float32)
    for c in range(NCORES):
        b, g = c // 2, c % 2
        oT = np.asarray(results[c]["outT"])
        out[b, g * HALF:(g + 1) * HALF, :] = oT.transpose(2, 1, 0).reshape(HALF, D)
    return out


_NC_CACHE = {}


def kernel(**inputs):
    maps = prep_inputs(**inputs)
    if "nc" not in _NC_CACHE:
        _NC_CACHE["nc"] = build_program()
    res = run_bass_kernel_spmd(_NC_CACHE["nc"], maps, core_ids=list(range(NCORES)))
    return assemble(res.results)
```
